# Optimizing a Trainium2 kernel written in Bass

```python
import jax, jax.numpy as jnp
from jax import lax
import numpy as np


D_MODEL = 2048
BATCH = 4
SEQ = 2048
DEPTH = 1
DEC_BATCH = 32
DEC_SEQ = 32
PAST_LEN = 1024

CHUNK = 64
D_SB = D_MODEL // 2
SB_HEAD_DIM = 128
SB_HEADS = D_SB // SB_HEAD_DIM
D_CONV = D_MODEL // 2
CONV_W = 3
D_MIX = D_SB + D_CONV
D_IN = 4 * D_SB + 4 * D_CONV
Q_BLOCK = 128
EPS = 1e-6

kernel_name = 'stick_break_shortconv_hybrid_step'


def _rmsnorm(x, g):
    x32 = x.astype(jnp.float32)
    r = lax.rsqrt(jnp.mean(x32 * x32, axis=-1, keepdims=True) + EPS)
    return (x32 * r).astype(x.dtype) * g


def _project(h, w_in):
    proj = jnp.einsum('btd,de->bte', h, w_in)
    cuts = [D_SB, 2 * D_SB, 3 * D_SB, 4 * D_SB,
            4 * D_SB + D_CONV, 4 * D_SB + 2 * D_CONV, 4 * D_SB + 3 * D_CONV]
    q, k, v, g_sb, b, c, u, g_cv = jnp.split(proj, cuts, axis=-1)
    bsz, t = h.shape[0], h.shape[1]
    heads = lambda a: a.reshape(bsz, t, SB_HEADS, SB_HEAD_DIM)
    return heads(q), heads(k), heads(v), g_sb, b, c, u, g_cv


def _sb_attend(q, k, v, q_pos, k_pos):
    scale = SB_HEAD_DIM ** -0.5
    z = jnp.einsum('bqhd,bkhd->bhqk', q, k, preferred_element_type=jnp.float32) * scale
    mask = k_pos[None, :] < q_pos[:, None]
    log_beta = jax.nn.log_sigmoid(z)
    log_1m = jnp.where(mask, jax.nn.log_sigmoid(-z), 0.0)
    suffix = lax.cumsum(log_1m, axis=3, reverse=True) - log_1m
    w = jnp.where(mask, jnp.exp(log_beta + suffix), 0.0)
    return jnp.einsum('bhqk,bkhd->bqhd', w.astype(v.dtype), v)


def _sb_prompt(q, k, v):
    bsz, t = q.shape[0], q.shape[1]
    nb = t // Q_BLOCK
    qb = q.reshape(bsz, nb, Q_BLOCK, SB_HEADS, SB_HEAD_DIM).transpose(1, 0, 2, 3, 4)
    q_pos = jnp.arange(nb)[:, None] * Q_BLOCK + jnp.arange(Q_BLOCK)[None, :]
    k_pos = jnp.arange(t)
    ob = lax.map(lambda a: _sb_attend(a[0], k, v, a[1], k_pos), (qb, q_pos))
    return ob.transpose(1, 0, 2, 3, 4).reshape(bsz, t, SB_HEADS, SB_HEAD_DIM)


def _short_conv(b, cu, left, conv_w):
    t = cu.shape[1]
    full = jnp.concatenate([left.astype(cu.dtype), cu], axis=1)
    conv = conv_w[0] * full[:, 0:t]
    for i in range(1, CONV_W):
        conv = conv + conv_w[i] * full[:, i:i + t]
    return b * conv, full[:, t:]


def _merge(o_sb, g_sb, o_cv, g_cv, w_out):
    bsz, t = o_sb.shape[0], o_sb.shape[1]
    mix = jnp.concatenate([o_sb.reshape(bsz, t, D_SB) * jax.nn.silu(g_sb),
                           o_cv * jax.nn.silu(g_cv)], axis=-1)
    return jnp.einsum('bte,ed->btd', mix, w_out)


def _prompt_layer(x, norm_g, w_in, conv_w, w_out):
    h = _rmsnorm(x, norm_g)
    q, k, v, g_sb, b, c, u, g_cv = _project(h, w_in)
    o_sb = _sb_prompt(q, k, v)
    left = jnp.zeros((x.shape[0], CONV_W - 1, D_CONV), x.dtype)
    o_cv, conv_tail = _short_conv(b, c * u, left, conv_w)
    return x + _merge(o_sb, g_sb, o_cv, g_cv, w_out), k, v, conv_tail


def _sample_layer(x, k_past, v_past, conv_past, norm_g, w_in, conv_w, w_out):
    h = _rmsnorm(x, norm_g)
    q, k, v, g_sb, b, c, u, g_cv = _project(h, w_in)
    past, t = k_past.shape[1], x.shape[1]
    k_all = jnp.concatenate([k_past.astype(k.dtype), k], axis=1)
    v_all = jnp.concatenate([v_past.astype(v.dtype), v], axis=1)
    q_pos = past + jnp.arange(t)
    k_pos = jnp.arange(past + t)
    o_sb = _sb_attend(q, k_all, v_all, q_pos, k_pos)
    o_cv, conv_tail = _short_conv(b, c * u, conv_past, conv_w)
    return x + _merge(o_sb, g_sb, o_cv, g_cv, w_out), k, v, conv_tail


def setup_inputs(seed: int = 0) -> dict:
    key = jax.random.key(seed)
    ks = jax.random.split(key, 10)
    f32 = jnp.float32
    return {
        'x_prompt': jax.random.normal(ks[0], (BATCH, SEQ, D_MODEL), f32),
        'x_sample': jax.random.normal(ks[1], (DEC_BATCH, DEC_SEQ, D_MODEL), f32),
        'cache_k': jax.random.normal(ks[2], (DEPTH, DEC_BATCH, PAST_LEN, SB_HEADS, SB_HEAD_DIM), f32),
        'cache_v': jax.random.normal(ks[3], (DEPTH, DEC_BATCH, PAST_LEN, SB_HEADS, SB_HEAD_DIM), f32),
        'state_conv': jax.random.normal(ks[4], (DEPTH, DEC_BATCH, CONV_W - 1, D_CONV), f32),
        'norm_g': 1.0 + 0.01 * jax.random.normal(ks[5], (DEPTH, D_MODEL), f32),
        'w_in': jax.random.normal(ks[6], (DEPTH, D_MODEL, D_IN), f32) * D_MODEL ** -0.5,
        'conv_w': jax.random.normal(ks[7], (DEPTH, CONV_W, D_CONV), f32) * CONV_W ** -0.5,
        'w_out': jax.random.normal(ks[8], (DEPTH, D_MIX, D_MODEL), f32) * D_MIX ** -0.5,
        'final_g': 1.0 + 0.01 * jax.random.normal(ks[9], (D_MODEL,), f32),
    }


def reference(x_prompt, x_sample, cache_k, cache_v, state_conv, norm_g, w_in, conv_w, w_out, final_g):
    xp, xs = x_prompt, x_sample
    kp_l, vp_l, cp_l, ks_l, vs_l, cs_l = [], [], [], [], [], []
    for layer in range(DEPTH):
        xp, kp, vp, cp = _prompt_layer(xp, norm_g[layer], w_in[layer], conv_w[layer], w_out[layer])
        xs, kn, vn, cn = _sample_layer(xs, cache_k[layer], cache_v[layer], state_conv[layer],
                                       norm_g[layer], w_in[layer], conv_w[layer], w_out[layer])
        kp_l.append(kp); vp_l.append(vp); cp_l.append(cp)
        ks_l.append(kn); vs_l.append(vn); cs_l.append(cn)
    y_prompt = _rmsnorm(xp, final_g)
    y_sample = _rmsnorm(xs, final_g)
    return (y_prompt, y_sample, jnp.stack(kp_l), jnp.stack(vp_l), jnp.stack(cp_l),
            jnp.stack(ks_l), jnp.stack(vs_l), jnp.stack(cs_l))
```

```python
import numpy as np
import concourse.bass as bass
import concourse.mybir as mybir
from concourse.bass_utils import run_bass_kernel_spmd

F32 = mybir.dt.float32
BF16 = mybir.dt.bfloat16
AF = mybir.ActivationFunctionType
ALU = mybir.AluOpType

NCORES = 8
D = 2048
NDC = 16
TC = 1024
TO = 1024
TS = 128
HALO0 = TO
SMP0 = TO + 2
TA = TO + 2 + TS
NH = 8
SCALE = 128.0 ** -0.5
NEG = -30000.0
NEGD = -30.0
EPS = 1e-6
SB_BASE = 16640
SB_END = 229376
NWSLOT = 8


class Sched:
    ENG = ["sync", "scalar", "vector", "gpsimd", "tensor"]

    def __init__(self, nc):
        self.nc = nc
        self.ops = {e: [] for e in self.ENG}
        self.sem = {e: nc.alloc_semaphore("prog_" + e) for e in self.ENG}
        self.cnt = {e: 0 for e in self.ENG}
        self.seen = {e: {} for e in self.ENG}
        self.dsem = {}
        self.dcnt = {}
        self.lastw = {}
        self.readers = {}
        self.pending = {e: [] for e in self.ENG}
        self.groups = {}

    def _deps_for(self, reads, writes):
        deps = []
        for k in reads:
            w = self.lastw.get(k)
            if w is not None:
                deps.append(w)
        for k in writes:
            w = self.lastw.get(k)
            if w is not None:
                deps.append(w)
            deps.extend(self.readers.get(k, {}).values())
        return deps

    def _waits(self, eng, deps):
        waits = []
        for d in deps:
            if d is None:
                continue
            if d[1] is None:
                assert d[2] == eng == "tensor", (d, eng)
                continue
            sem, val = d[0], d[1]
            k = sem.name
            if self.seen[eng].get(k, 0) >= val:
                continue
            self.seen[eng][k] = val
            waits.append((sem, val))
        return waits

    def _register(self, tok, reads, writes):
        for k in reads:
            r = self.readers.setdefault(k, {})
            name = tok[0].name if tok[0] is not None else ("pend_" + tok[2])
            r[name] = tok
        for k in writes:
            self.lastw[k] = tok
            self.readers[k] = {}

    def op(self, eng, fn, reads=(), writes=(), inc=True, extra=()):
        deps = self._deps_for(reads, writes) + [x for x in extra if x is not None]
        waits = self._waits(eng, deps)
        if inc:
            self.cnt[eng] += 1
            tok = [self.sem[eng], self.cnt[eng], eng]
            for p in self.pending[eng]:
                p[0], p[1] = tok[0], tok[1]
            self.pending[eng] = []
        else:
            assert eng == "tensor"
            tok = [None, None, eng]
            self.pending[eng].append(tok)
        self._register(tok, reads, writes)
        self.ops[eng].append((fn, waits, tok if inc else None, 1))
        return tok

    def dma(self, eng, key, out, in_, reads=(), writes=(), extra=(), small=False):
        if key not in self.dsem:
            self.dsem[key] = self.nc.alloc_semaphore("dma_" + key)
            self.dcnt[key] = 0
        deps = self._deps_for(reads, writes) + [x for x in extra if x is not None]
        if key not in self.groups and self.dcnt[key] > 0:
            deps.append([self.dsem[key], self.dcnt[key], "dma"])
        waits = self._waits(eng, deps)
        self.dcnt[key] += 16
        tok = [self.dsem[key], self.dcnt[key], "dma"]
        if key in self.groups:
            self.groups[key].append(tok)
        self._register(tok, reads, writes)
        nc = self.nc

        def fn(e):
            if small:
                with nc.allow_non_contiguous_dma(reason="tiny strided transfer"):
                    return e.dma_start(out=out, in_=in_)
            return e.dma_start(out=out, in_=in_)
        self.ops[eng].append((fn, waits, tok, 16))
        return tok

    def group_begin(self, eng, key):
        if key not in self.dsem:
            self.dsem[key] = self.nc.alloc_semaphore("dma_" + key)
            self.dcnt[key] = 0
        assert key not in self.groups
        self.groups[key] = []
        if self.dcnt[key] > 0:
            w = self._waits(eng, [[self.dsem[key], self.dcnt[key], "dma"]])
            if w:
                self.ops[eng].append((lambda e: e.nop(), w, None, 0))

    def group_end(self, key):
        for t in self.groups.pop(key):
            t[1] = self.dcnt[key]

    def snapshot(self):
        toks = []
        assert not self.groups, "open DMA group at snapshot"
        for e in self.ENG:
            assert not self.pending[e], "pending PE tokens at snapshot"
            if self.cnt[e] > 0:
                toks.append([self.sem[e], self.cnt[e], e])
        for k, s in self.dsem.items():
            toks.append([s, self.dcnt[k], "dma"])
        return toks

    def all_dma_tokens(self):
        return [[s, self.dcnt[k], "dma"] for k, s in self.dsem.items()]

    def emit(self, block):
        final = self.all_dma_tokens()

        def mk(ename):
            def body(e):
                for fn, waits, tok, n in self.ops[ename]:
                    for sem, val in waits:
                        e.wait_ge(sem, val)
                    ins = fn(e)
                    if tok is not None:
                        ins.then_inc(tok[0], n)
                if ename == "sync":
                    for sem, val, _ in final:
                        e.wait_ge(sem, val)
            return body
        block.sync(mk("sync"))
        block.scalar(mk("scalar"))
        block.vector(mk("vector"))
        block.gpsimd(mk("gpsimd"))
        block.tensor(mk("tensor"))


class Arena:
    def __init__(self, nc):
        self.nc = nc
        self.n = 0

    def at(self, name, shape, dt, off):
        sz = int(np.prod(shape[1:])) * (4 if dt == F32 else 2)
        assert off % 32 == 0, (name, off)
        assert SB_BASE <= off and off + sz <= SB_END, (name, off, sz)
        self.n += 1
        return self.nc.alloc_sbuf_tensor_at(f"{name}_{self.n}", list(shape), dt, offset=off)


def _forder():
    fo = []
    for j in range(8):
        fo += [("c", j, 40 + j), ("u", j, 48 + j), ("b", j, 32 + j), ("gcv", j, 56 + j)]
        fo += [("k", j, 8 + j)]
    fo += [("v", h, 16 + h) for h in range(8)]
    fo += [("q", h, h) for h in range(8)]
    fo += [("gsb", h, 24 + h) for h in range(8)]
    return fo


def _forder_cols():
    return [f for (_, _, f) in _forder()]


def build_program():
    nc = bass.Bass("TRN2", target_bir_lowering=False)
    S = _build_body(nc)
    with nc.Block() as block:
        S.emit(block)
    return nc


def _build_body(nc):
    S = Sched(nc)
    A = Arena(nc)

    def din(name, shape):
        return nc.dram_tensor(name, list(shape), F32, kind="ExternalInput").ap()

    def dout(name, shape):
        return nc.dram_tensor(name, list(shape), F32, kind="ExternalOutput").ap()

    xa = din("xa", [TC + TO + TS, D])
    ck = din("ck", [4, 1024, 1024])
    cv = din("cv", [4, 1024, 1024])
    sc = din("sc", [4, 2, 1024])
    norm_g = din("norm_g", [D])
    final_g = din("final_g", [D])
    conv_w = din("conv_w", [3, 1024])
    w_in = din("w_in", [64, 128, NDC * 128])
    w_out = din("w_out", [4, 128, NDC * 512])
    yp = dout("yp", [TO, D])
    ys = dout("ys", [TS, D])
    kp = dout("kp", [TO, 1024])
    vp = dout("vp", [TO, 1024])
    cp = dout("cp", [2, 1024])
    ksn = dout("ksn", [TS, 1024])
    vsn = dout("vsn", [TS, 1024])
    cs = dout("cs", [4, 2, 1024])

    o = SB_BASE
    O_CONST = o; o += 6144
    O_R1 = o; o += 36928
    O_VT = o; o += 32768
    O_KT = o; o += 32800
    O_KTS = o; o += 4096
    O_MCV = o; o += 18464
    O_R2A = o; o += 18464
    O_R2B = o; o += 18464
    O_WR = o; o += NWSLOT * 4096
    O_TR = o
    assert O_TR + 10240 + 512 <= SB_END, O_TR
    O_ATT = O_R2B
    ATT_SZ = SB_END - O_ATT

    c = O_CONST
    ident_bf = A.at("ident_bf", [128, 128], BF16, c); c += 256
    ident_f = A.at("ident_f", [128, 128], F32, c); c += 512
    maskTd = A.at("maskTd", [128, 128], BF16, c); c += 256
    maskTs = []
    for s in range(4):
        maskTs.append(A.at(f"maskTs{s}", [128, 128], BF16, c)); c += 256
    ones = A.at("ones", [128, 512], F32, c); c += 2048
    convw = A.at("convw", [128, 8, 3], F32, c); c += 128
    scT = A.at("scT", [128, 8, 4, 2], F32, c); c += 256
    stat = A.at("stat", [128, 96], F32, c); c += 384
    negT = A.at("negT", [128, 2], F32, c); c += 32
    mtmp = A.at("mtmp", [128, 128], F32, c); c += 512
    nident_f = A.at("nident_f", [128, 128], F32, c); c += 512
    assert c <= O_CONST + 6144

    hTC = A.at("hTC", [128, NDC, TC], BF16, O_R1)
    qT = A.at("qT", [128, NH, TA], BF16, O_R1)
    gsb = A.at("gsb", [128, NH, TA], BF16, O_R1 + 18464)
    hTA = A.at("hTA", [128, NDC, TA], BF16, O_R2A)
    mixsb = A.at("mixsb", [128, NH, TA], BF16, O_R2A)
    v_hm = A.at("v_hm", [128, NH, 16, 128], BF16, O_VT)
    kT = A.at("kT", [128, NH, 2050], BF16, O_KT)
    kT_s = A.at("kT_s", [128, NH, 128], BF16, O_KTS)
    v_s = A.at("v_s", [128, 1024], BF16, O_KTS + 2048)
    mixcv = A.at("mixcv", [128, NH, TA], BF16, O_MCV)
    wring = [A.at(f"wr{i}", [128, NDC, 128], BF16, O_WR + 4096 * i) for i in range(NWSLOT)]
    kfm = [A.at(f"kfm{i}", [128, 512], F32, O_TR + 2048 * i) for i in range(2)]
    ktm = [A.at(f"ktm{i}", [128, 4, 128], F32, O_TR + 4096 + 2048 * i) for i in range(3)]
    xt = [A.at(f"xt{i}", [128, D], F32, O_KT + 8192 * i) for i in range(2)]
    xt.append(A.at("xt2", [128, D], F32, O_MCV))
    xs = [A.at(f"xs{i}", [128, D], BF16, O_KT + 16384 + 4096 * i) for i in range(2)]
    junk = A.at("junk", [128, D], BF16, O_KT + 24576)
    gbc = A.at("gbc", [128, D], F32, O_KT + 28672)
    assert 28672 + 8192 <= 32800 + 4096
    c_sb = A.at("c_sb", [128, TA], F32, O_VT)
    cu_p = A.at("cu_p", [128, 1026], F32, O_VT + 4640)
    cu_s = A.at("cu_s", [128, 4, 34], F32, O_VT + 4640 + 4128)
    acc = A.at("acc", [128, 1152], F32, O_VT + 4640 + 4128 + 576)
    sg = A.at("sg", [128, TA], F32, O_VT + 4640 + 4128 + 576 + 4608)
    b_sb = A.at("b_sb", [128, TA], F32, O_VT + 4640 + 4128 + 576 + 4608 + 4640)
    a = O_ATT
    spb = []; gb_ = []; wfull = []
    for i in range(3):
        spb.append(A.at(f"sp{i}", [128, 512], F32, a)); a += 2048
    for i in range(4):
        gb_.append(A.at(f"g{i}", [128, 512], BF16, a)); a += 1024
    for i in range(2):
        wfull.append(A.at(f"wf{i}", [128, 2048], BF16, a)); a += 4096
    wTfull = A.at("wTfull", [128, 17, 128], BF16, a)
    g_s = A.at("g_s", [128, NH, 128], BF16, a + 2304)
    a += 4352
    a_ph = a
    nlbS = []; ubuf = []
    for i in range(2):
        ubuf.append(A.at(f"u{i}", [128, 2048], F32, a)); a += 8192
        nlbS.append(A.at(f"nlbS{i}", [128, 2056], F32, a)); a += 8224
    twin = A.at("twin", [128, 128], F32, a); a += 512
    tjunk = A.at("tjunk", [128, 128], F32, a); a += 512
    assert a <= SB_END, a
    a = a_ph
    esb = []; psb = []
    for i in range(2):
        esb.append(A.at(f"es{i}", [128, 1152], F32, a)); a += 4608
        psb.append(A.at(f"Ps{i}", [128, 1160], F32, a)); a += 4640
    kTp = A.at("kTp", [128, NH, 1024], BF16, a); a += 16384
    q_s = A.at("q_s", [128, NH, 128], BF16, a); a += 2048
    assert a <= SB_END, a
    ckb = [A.at(f"ckb{i}", [128, 8, 1024], BF16, O_VT + 16384 * i) for i in range(2)]
    cvb = [A.at(f"cvb{i}", [128, 8, 1024], BF16, O_VT + 32768 + 16384 * i) for i in range(2)]
    assert O_VT + 65536 <= O_KTS
    wo = [A.at(f"wo{i}", [128, NDC, 512], BF16, O_R1 + 16384 * i) for i in range(2)]
    yacc8 = A.at("yacc8", [128, 8, D], F32, O_VT)
    assert O_VT + 65536 <= O_MCV
    fgb = A.at("fgb", [128, D], F32, O_ATT)
    yacc_s = A.at("yacc_s", [128, D], F32, O_ATT + 12288)

    def yv(tt, lo=0, hi=D):
        return yacc8[:, tt, lo:hi] if tt < 8 else yacc_s[:, lo:hi]
    junk2 = A.at("junk2", [128, D], BF16, O_ATT + 8192)

    pA = [nc.alloc_psum_tensor(f"pA{i}", [128, 512], F32) for i in range(4)]
    pTf = [nc.alloc_psum_tensor(f"pTf{i}", [128, 4, 128], F32) for i in range(2)]
    pTb = [nc.alloc_psum_tensor(f"pTb{i}", [128, 8, 128], BF16) for i in range(2)]

    ring = {"pA": 0, "pTf": 0, "pTb": 0, "w": 0, "kfm": 0, "ktm": 0, "sp": 0, "g": 0, "x": 0, "xs": 0}

    def nxt(name, n):
        i = ring[name] % n
        ring[name] += 1
        return i

    S.op("gpsimd", lambda e: e.memset(mtmp[:], 0.0), writes=["mtmp"])
    S.op("gpsimd", lambda e: e.affine_select(out=ident_f[:], in_=mtmp[:], pattern=[[-1, 128]],
                                             compare_op=ALU.not_equal, fill=1.0, base=0,
                                             channel_multiplier=1), reads=["mtmp"], writes=["ident_f"])
    S.op("gpsimd", lambda e: e.tensor_copy(out=ident_bf[:], in_=ident_f[:]), reads=["ident_f"], writes=["ident_bf"])
    S.op("gpsimd", lambda e: e.affine_select(out=maskTd[:], in_=mtmp[:], pattern=[[1, 128]],
                                             compare_op=ALU.is_gt, fill=NEGD, base=0,
                                             channel_multiplier=-1), reads=["mtmp"], writes=["maskTd"])
    S.op("gpsimd", lambda e: e.affine_select(out=nident_f[:], in_=mtmp[:], pattern=[[-1, 128]],
                                             compare_op=ALU.not_equal, fill=-1.0, base=0,
                                             channel_multiplier=1), reads=["mtmp"], writes=["nident_f"])
    S.op("gpsimd", lambda e: e.memset(ones[:], 1.0), writes=["ones"])

    msk_tmp = A.at("msk_tmp", [128, 128], F32, O_TR + 10240)
    for s in range(4):
        S.op("gpsimd", lambda e, s=s: e.affine_select(out=msk_tmp[:], in_=mtmp[:], pattern=[[0, 4], [0, 32]],
                                                      compare_op=ALU.is_ge, fill=NEG, base=-32 * s,
                                                      channel_multiplier=1),
             reads=["mtmp", ("maskTs", s - 1)], writes=["msk_tmp"])
        S.op("gpsimd", lambda e, s=s: e.affine_select(out=msk_tmp[:], in_=msk_tmp[:], pattern=[[0, 4], [1, 32]],
                                                      compare_op=ALU.is_gt, fill=NEG, base=32 * s,
                                                      channel_multiplier=-1),
             reads=["msk_tmp"], writes=["msk_tmp"])
        S.op("gpsimd", lambda e, s=s: e.tensor_copy(out=maskTs[s][:], in_=msk_tmp[:]),
             reads=["msk_tmp"], writes=[("maskTs", s)])
    S.dma("sync", "gbc", gbc[:], norm_g.partition_broadcast(128), writes=["gbc"])
    S.group_begin("gpsimd", "const")
    for j in range(8):
        S.dma("gpsimd", "const", convw[:, j, :], conv_w[:, j * 128:(j + 1) * 128].rearrange("i p -> p i"),
              writes=[("convw", j)], small=True)
        for s in range(4):
            S.dma("gpsimd", "const", scT[:, j, s, :], sc[s, :, j * 128:(j + 1) * 128].rearrange("r p -> p r"),
                  writes=[("scT", j, s)], small=True)
    S.group_end("const")


    forder = _forder()
    wload_state = {"next": 0, "pending_cast": []}

    def issue_wload(extra=()):
        i = wload_state["next"]
        if i >= len(forder):
            return
        wload_state["next"] += 1
        slot = i % NWSLOT
        S.dma("gpsimd", f"w{slot}", wring[slot][:].rearrange("p c f -> p (c f)"), w_in[i],
              writes=[("w", slot)], extra=extra)

    for _ in range(NWSLOT):
        issue_wload()

    tile_order = [7] + list(range(8, 17)) + list(range(0, 7))
    for n_i, tt in enumerate(tile_order):
        xi = nxt("x", 3)
        S.dma("sync", f"x{xi}", xt[xi][:], xa[tt * 128:(tt + 1) * 128, :], writes=[("xt", xi)])
        S.op("scalar", lambda e, xi=xi, tt=tt: e.activation(out=junk[:], in_=xt[xi][:], func=AF.Square,
                                                            accum_out=stat[:, tt:tt + 1]),
             reads=[("xt", xi)], writes=["junk", ("ss", tt)])
        S.op("scalar", lambda e, tt=tt: e.activation(out=stat[:, 32 + tt:33 + tt], in_=stat[:, tt:tt + 1],
                                                     func=AF.Ln, scale=1.0 / D, bias=EPS),
             reads=[("ss", tt)], writes=[("lr", tt)])
        S.op("scalar", lambda e, tt=tt: e.activation(out=stat[:, 64 + tt:65 + tt], in_=stat[:, 32 + tt:33 + tt],
                                                     func=AF.Exp, scale=-0.5),
             reads=[("lr", tt)], writes=[("rr", tt)])
        si = nxt("xs", 2)
        S.op("vector", lambda e, xi=xi, si=si, tt=tt: e.scalar_tensor_tensor(
            out=xs[si][:], in0=xt[xi][:], scalar=stat[:, 64 + tt:65 + tt], in1=gbc[:],
            op0=ALU.mult, op1=ALU.mult),
            reads=[("xt", xi), ("rr", tt), "gbc"], writes=[("xs", si)])
        for grp in range(4):
            pi = nxt("pTb", 2)
            for jj in range(4):
                dc = grp * 4 + jj
                S.op("tensor", lambda e, si=si, dc=dc, pi=pi, jj=jj: e.transpose(
                    out=pTb[pi][:, jj, :], in_=xs[si][:, dc * 128:(dc + 1) * 128], identity=ident_bf[:]),
                    reads=[("xs", si), "ident_bf"], writes=[("pTb", pi)], inc=(jj == 3))
            if tt < 8:
                dst = hTC[:, grp * 4:(grp + 1) * 4, tt * 128:(tt + 1) * 128]
                wkey = ("hTC", tt)
            elif tt < 16:
                dst = hTA[:, grp * 4:(grp + 1) * 4, (tt - 8) * 128:(tt - 7) * 128]
                wkey = ("hTA", tt - 8)
            else:
                dst = hTA[:, grp * 4:(grp + 1) * 4, SMP0:SMP0 + 128]
                wkey = ("hTA", 8)
            ceng = "vector"
            if ceng == "vector":
                S.op("vector", lambda e, dst=dst, pi=pi: e.tensor_copy(out=dst, in_=pTb[pi][:, 0:4, :]),
                     reads=[("pTb", pi)], writes=[wkey])
            else:
                S.op("scalar", lambda e, dst=dst, pi=pi: e.activation(out=dst, in_=pTb[pi][:, 0:4, :], func=AF.Copy),
                     reads=[("pTb", pi)], writes=[wkey])
            if tt == 7:
                hdst = hTA[:, grp * 4:(grp + 1) * 4, HALO0:HALO0 + 2]
                S.op("vector", lambda e, hdst=hdst, pi=pi: e.tensor_copy(out=hdst, in_=pTb[pi][:, 0:4, 126:128]),
                     reads=[("pTb", pi)], writes=[("hTA", 8)])
    t_norm_done = S.snapshot()

    def hkeys(buf, c0, w):
        if buf == "A":
            lo = min(c0 // 128, 8); hi = min((c0 + w - 1) // 128, 8)
            return [("hTA", t) for t in range(lo, hi + 1)]
        lo = c0 // 128; hi = (c0 + w - 1) // 128
        return [("hTC", t) for t in range(lo, hi + 1)]

    A_CHUNKS_KV = [(0, 512), (512, 512), (1024, TA - 1024)]
    A_CHUNKS_BAL = [(0, 385), (385, 385), (770, TA - 770)]
    C_CHUNKS = [(0, 512), (512, 512)]

    defer_q = []

    def proj_tile(slot, buf, c0, w):
        pi = nxt("pA", 4)
        src = hTA if buf == "A" else hTC
        rk = hkeys(buf, c0, w) + [("w", slot)]
        for dc in range(NDC):
            S.op("tensor", lambda e, pi=pi, slot=slot, dc=dc, src=src, c0=c0, w=w: e.matmul(
                pA[pi][:, 0:w], lhsT=wring[slot][:, dc, :], rhs=src[:, dc, c0:c0 + w],
                start=(dc == 0), stop=(dc == NDC - 1)),
                reads=rk, writes=[("pA", pi)], inc=(dc == NDC - 1))
        while defer_q:
            defer_q.pop(0)()
        return pi

    for ci, (kind, idx, f) in enumerate(forder):
        slot = ci % NWSLOT
        if ci >= 1:
            issue_wload()
        extra_first = ()
        if kind == "k" and idx == 0:
            extra_first = t_norm_done
        if kind == "v" and idx == 0:
            extra_first = S.snapshot()
        if kind == "q" and idx == 0:
            extra_first = S.snapshot()
        if kind in ("k", "v"):
            for (c0, w) in C_CHUNKS:
                pi = proj_tile(slot, "C", c0, w)
                if kind == "k":
                    S.op("scalar", lambda e, pi=pi, idx=idx, c0=c0, w=w: e.activation(
                        out=kT[:, idx, c0:c0 + w], in_=pA[pi][:, 0:w], func=AF.Copy),
                        reads=[("pA", pi)], writes=[("kT", idx)], extra=extra_first)
                    extra_first = ()
                else:
                    ki = nxt("kfm", 2)
                    S.op("vector", lambda e, pi=pi, ki=ki, w=w: e.tensor_copy(out=kfm[ki][:, 0:w], in_=pA[pi][:, 0:w]),
                         reads=[("pA", pi)], writes=[("kfm", ki)])
                    def post_ctx(ki=ki, c0=c0, idx=idx, ex=extra_first):
                        ti = nxt("pTf", 2)
                        for b in range(4):
                            S.op("tensor", lambda e, ki=ki, ti=ti, b=b: e.transpose(
                                out=pTf[ti][:, b, :], in_=kfm[ki][:, b * 128:(b + 1) * 128], identity=ident_f[:]),
                                reads=[("kfm", ki), "ident_f"], writes=[("pTf", ti)], inc=(b == 3))
                        t0 = c0 // 128
                        S.op("vector", lambda e, ti=ti, t0=t0, idx=idx: e.tensor_copy(
                            out=v_hm[:, idx, t0:t0 + 4, :], in_=pTf[ti][:, 0:4, :]),
                            reads=[("pTf", ti)], writes=[("v", t) for t in range(t0, t0 + 4)], extra=ex)
                    defer_q.append(post_ctx)
                    extra_first = ()
        A_CHUNKS = A_CHUNKS_KV if kind in ("k", "v") else A_CHUNKS_BAL
        for cc, (c0, w) in enumerate(A_CHUNKS):
            pi = proj_tile(slot, "A", c0, w)
            own_len = max(0, min(c0 + w, TO) - c0)
            has_tail = (c0 + w == TA)
            if kind == "c":
                S.op("scalar", lambda e, pi=pi, c0=c0, w=w: e.activation(
                    out=c_sb[:, c0:c0 + w], in_=pA[pi][:, 0:w], func=AF.Copy),
                    reads=[("pA", pi)], writes=[("c_sb", cc)])
            elif kind == "u":
                S.op("vector", lambda e, pi=pi, c0=c0, n_=own_len: e.tensor_tensor(
                    out=cu_p[:, 2 + c0:2 + c0 + n_], in0=pA[pi][:, 0:n_], in1=c_sb[:, c0:c0 + n_], op=ALU.mult),
                    reads=[("pA", pi), ("c_sb", cc)], writes=[("cu", cc)])
                if has_tail:
                    ho = HALO0 - c0
                    so = SMP0 - c0
                    S.op("vector", lambda e, pi=pi, ho=ho: e.tensor_tensor(
                        out=cu_p[:, 0:2], in0=pA[pi][:, ho:ho + 2], in1=c_sb[:, HALO0:HALO0 + 2], op=ALU.mult),
                        reads=[("pA", pi), ("c_sb", 2)], writes=[("cu", 5)])
                    S.op("gpsimd", lambda e, idx=idx: e.tensor_copy(out=cu_s[:, :, 0:2], in_=scT[:, idx, :, :]),
                         reads=[("scT", idx, s_) for s_ in range(4)], writes=[("cu", 3)])
                    S.op("vector", lambda e, pi=pi, so=so: e.tensor_tensor(
                        out=cu_s[:, :, 2:34], in0=pA[pi][:, so:so + 128].rearrange("p (s t) -> p s t", s=4),
                        in1=c_sb[:, SMP0:SMP0 + 128].rearrange("p (s t) -> p s t", s=4), op=ALU.mult),
                        reads=[("pA", pi), ("c_sb", 2)], writes=[("cu", 4)])
                    S.group_begin("sync", "cout")
                    S.dma("sync", "cout", cp[:, idx * 128:(idx + 1) * 128].rearrange("r p -> p r"),
                          cu_p[:, 1024:1026], reads=[("cu", 2)], small=True)
                    for s in range(4):
                        S.dma("sync", "cout", cs[s, :, idx * 128:(idx + 1) * 128].rearrange("r p -> p r"),
                              cu_s[:, s, 32:34], reads=[("cu", 4)], small=True)
                    S.group_end("cout")
                    allcu = [("cu", i) for i in range(6)]
                    S.op("vector", lambda e, idx=idx: e.tensor_scalar(
                        out=acc[:, 0:1024], in0=cu_p[:, 2:1026], scalar1=convw[:, idx, 2:3], scalar2=None,
                        op0=ALU.mult), reads=allcu + [("convw", idx)], writes=["acc"])
                    S.op("vector", lambda e, idx=idx: e.scalar_tensor_tensor(
                        out=acc[:, 0:1024], in0=cu_p[:, 1:1025], scalar=convw[:, idx, 1:2], in1=acc[:, 0:1024],
                        op0=ALU.mult, op1=ALU.add), reads=allcu + ["acc"], writes=["acc"])
                    S.op("vector", lambda e, idx=idx: e.scalar_tensor_tensor(
                        out=acc[:, 0:1024], in0=cu_p[:, 0:1024], scalar=convw[:, idx, 0:1], in1=acc[:, 0:1024],
                        op0=ALU.mult, op1=ALU.add), reads=allcu + ["acc"], writes=["acc"])
                    accs = acc[:, 1024:1152].rearrange("p (s t) -> p s t", s=4)
                    S.op("vector", lambda e, idx=idx, accs=accs: e.tensor_scalar(
                        out=accs, in0=cu_s[:, :, 2:34], scalar1=convw[:, idx, 2:3], scalar2=None,
                        op0=ALU.mult), reads=allcu + ["acc"], writes=["acc"])
                    S.op("vector", lambda e, idx=idx, accs=accs: e.scalar_tensor_tensor(
                        out=accs, in0=cu_s[:, :, 1:33], scalar=convw[:, idx, 1:2], in1=accs,
                        op0=ALU.mult, op1=ALU.add), reads=allcu + ["acc"], writes=["acc"])
                    S.op("vector", lambda e, idx=idx, accs=accs: e.scalar_tensor_tensor(
                        out=accs, in0=cu_s[:, :, 0:32], scalar=convw[:, idx, 0:1], in1=accs,
                        op0=ALU.mult, op1=ALU.add), reads=allcu + ["acc"], writes=["acc"])
            elif kind == "b":
                S.op("scalar", lambda e, pi=pi, c0=c0, w=w: e.activation(
                    out=b_sb[:, c0:c0 + w], in_=pA[pi][:, 0:w], func=AF.Copy),
                    reads=[("pA", pi)], writes=[("b_sb", cc)])
                if has_tail:
                    S.op("gpsimd", lambda e: e.tensor_tensor(
                        out=acc[:, 0:1024], in0=acc[:, 0:1024], in1=b_sb[:, 0:1024], op=ALU.mult),
                        reads=["acc", ("b_sb", 0), ("b_sb", 1), ("b_sb", 2)], writes=["acc"])
                    S.op("gpsimd", lambda e: e.tensor_tensor(
                        out=acc[:, 1024:1152], in0=acc[:, 1024:1152], in1=b_sb[:, SMP0:SMP0 + 128], op=ALU.mult),
                        reads=["acc", ("b_sb", 2)], writes=["acc"])
            elif kind == "gcv":
                S.op("scalar", lambda e, pi=pi, c0=c0, w=w: e.activation(
                    out=sg[:, c0:c0 + w], in_=pA[pi][:, 0:w], func=AF.Silu),
                    reads=[("pA", pi)], writes=[("sg", cc)])
                if has_tail:
                    S.op("gpsimd", lambda e, idx=idx: e.tensor_tensor(
                        out=mixcv[:, idx, 0:1024], in0=acc[:, 0:1024], in1=sg[:, 0:1024], op=ALU.mult),
                        reads=["acc", ("sg", 0), ("sg", 1), ("sg", 2)], writes=[("mixcv", idx)],
                        extra=(t_norm_done if idx == 0 else ()))
                    S.op("gpsimd", lambda e, idx=idx: e.tensor_tensor(
                        out=mixcv[:, idx, SMP0:SMP0 + 128], in0=acc[:, 1024:1152], in1=sg[:, SMP0:SMP0 + 128], op=ALU.mult),
                        reads=["acc", ("sg", 2)], writes=[("mixcv", idx)])
            elif kind == "q":
                S.op("scalar", lambda e, pi=pi, idx=idx, c0=c0, w=w: e.activation(
                    out=qT[:, idx, c0:c0 + w], in_=pA[pi][:, 0:w], func=AF.Copy, scale=SCALE),
                    reads=[("pA", pi)], writes=[("qT", idx)], extra=extra_first)
                extra_first = ()
            elif kind == "gsb":
                S.op("scalar", lambda e, pi=pi, idx=idx, c0=c0, w=w: e.activation(
                    out=gsb[:, idx, c0:c0 + w], in_=pA[pi][:, 0:w], func=AF.Silu),
                    reads=[("pA", pi)], writes=[("gsb", idx)])
            elif kind in ("k", "v"):
                nb = 4 if cc < 2 else 1
                pc0 = 0 if cc < 2 else 2
                pw = 512 if cc < 2 else 128
                if kind == "k":
                    if cc < 2:
                        S.op("scalar", lambda e, pi=pi, idx=idx, c0=c0: e.activation(
                            out=kT[:, idx, 1024 + c0:1024 + c0 + 512], in_=pA[pi][:, 0:512], func=AF.Copy),
                            reads=[("pA", pi)], writes=[("kT", idx)])
                    else:
                        S.op("scalar", lambda e, pi=pi, idx=idx: e.activation(
                            out=kT_s[:, idx, :], in_=pA[pi][:, 2:130], func=AF.Copy),
                            reads=[("pA", pi)], writes=[("kT_s", idx)])
                ki = nxt("kfm", 2)
                S.op("scalar", lambda e, pi=pi, ki=ki, pc0=pc0, pw=pw: e.activation(
                    out=kfm[ki][:, 0:pw], in_=pA[pi][:, pc0:pc0 + pw], func=AF.Copy),
                    reads=[("pA", pi)], writes=[("kfm", ki)])
                def post_own(ki=ki, nb=nb, kind=kind, cc=cc, c0=c0, idx=idx):
                    ti = nxt("pTf", 2)
                    for b in range(nb):
                        S.op("tensor", lambda e, ki=ki, ti=ti, b=b: e.transpose(
                            out=pTf[ti][:, b, :], in_=kfm[ki][:, b * 128:(b + 1) * 128], identity=ident_f[:]),
                            reads=[("kfm", ki), "ident_f"], writes=[("pTf", ti)], inc=(b == nb - 1))
                    if kind == "v":
                        if cc < 2:
                            t0 = 8 + c0 // 128
                            S.op("vector", lambda e, ti=ti, t0=t0, idx=idx: e.tensor_copy(
                                out=v_hm[:, idx, t0:t0 + 4, :], in_=pTf[ti][:, 0:4, :]),
                                reads=[("pTf", ti)], writes=[("v", t) for t in range(t0, t0 + 4)])
                        else:
                            S.op("vector", lambda e, ti=ti, idx=idx: e.tensor_copy(
                                out=v_s[:, idx * 128:(idx + 1) * 128], in_=pTf[ti][:, 0, :]),
                                reads=[("pTf", ti)], writes=[("v_s", idx)])
                    mi = nxt("ktm", 3)
                    S.op("vector", lambda e, ti=ti, mi=mi, nb=nb: e.tensor_copy(
                        out=ktm[mi][:, 0:nb, :], in_=pTf[ti][:, 0:nb, :]),
                        reads=[("pTf", ti)], writes=[("ktm", mi)])
                    if cc < 2:
                        dst = (kp if kind == "k" else vp)[c0:c0 + 512, idx * 128:(idx + 1) * 128].rearrange(
                            "(b p) f -> p b f", p=128)
                        S.dma("sync", f"ktm{mi}", dst, ktm[mi][:], reads=[("ktm", mi)])
                    else:
                        dst = (ksn if kind == "k" else vsn)[:, idx * 128:(idx + 1) * 128]
                        S.dma("sync", f"ktm{mi}", dst, ktm[mi][:, 0, :], reads=[("ktm", mi)])
                defer_q.append(post_own)

    while defer_q:
        defer_q.pop(0)()
    t_proj_done = S.snapshot()

    def attn_s1a(u):
        par = u["par"]
        ebuf, ek = u["e"], u["ek"]
        chunks = u["chunks"]
        if u.get("pre"):
            u["pre"]()
        for ci, (c0, w) in enumerate(chunks):
            pi = nxt("pA", 4)
            last = (ci == len(chunks) - 1)
            u["qk"](pi, ci, c0, w, last)
            S.op("scalar", lambda e, pi=pi, par=par, c0=c0, w=w: e.activation(
                out=ebuf[par][:, c0:c0 + w], in_=pA[pi][:, 0:w], func=AF.Exp, scale=1.0),
                reads=[("pA", pi)], writes=[(ek, par, ci)], extra=u.get("extra", ()))
        u["spi"] = []
        for ci, (c0, w) in enumerate(chunks):
            si = nxt("sp", 3)
            u["spi"].append(si)
            S.op("scalar", lambda e, si=si, par=par, c0=c0, w=w: e.activation(
                out=spb[si][:, 0:w], in_=ebuf[par][:, c0:c0 + w], func=AF.Ln, bias=1.0, scale=1.0),
                reads=[(ek, par, ci)], writes=[("sp", si)])

    def attn_s1b(u):
        par = u["par"]
        pbuf, pk = u["P"], u["pk"]
        for ci, (c0, w) in enumerate(u["chunks"]):
            si = u["spi"][ci]
            if ci == 0:
                S.op("vector", lambda e, si=si, par=par, w=w: e.tensor_tensor_scan(
                    out=pbuf[par][:, 1:1 + w], data0=ones[:, 0:w], data1=spb[si][:, 0:w], initial=0.0,
                    op0=ALU.mult, op1=ALU.add),
                    reads=[("sp", si), "ones"], writes=[(pk, par)])
            else:
                S.op("vector", lambda e, si=si, par=par, c0=c0, w=w: e.tensor_tensor_scan(
                    out=pbuf[par][:, 1 + c0:1 + c0 + w], data0=ones[:, 0:w], data1=spb[si][:, 0:w],
                    initial=pbuf[par][:, c0:c0 + 1], op0=ALU.mult, op1=ALU.add),
                    reads=[("sp", si), "ones", (pk, par)], writes=[(pk, par)])

    def attn_s2(u):
        par = u["par"]
        ebuf, pbuf, ek, pk = u["e"], u["P"], u["ek"], u["pk"]
        nk = u["nk"]
        S.op("scalar", lambda e, par=par, nk=nk: e.activation(
            out=negT[:, par:par + 1], in_=pbuf[par][:, nk:nk + 1], func=AF.Copy, scale=-1.0),
            reads=[(pk, par)], writes=[("negT", par)])
        for ci, (c0, w) in enumerate(u["chunks"]):
            gi = nxt("g", 4)
            S.op("scalar", lambda e, gi=gi, par=par, c0=c0, w=w: e.activation(
                out=gb_[gi][:, 0:w], in_=pbuf[par][:, c0:c0 + w], func=AF.Exp, bias=negT[:, par:par + 1], scale=1.0),
                reads=[(pk, par), ("negT", par)], writes=[("g", gi)])
            S.op("vector", lambda e, gi=gi, par=par, c0=c0, w=w: e.tensor_tensor(
                out=wfull[par][:, c0:c0 + w], in0=ebuf[par][:, c0:c0 + w], in1=gb_[gi][:, 0:w], op=ALU.mult),
                reads=[(ek, par, ci), ("g", gi)], writes=[("wf", par, ci)])

    def attn_s3(u):
        par = u["par"]
        oi = nxt("pTf", 2)
        u["oi"] = oi
        chunks = u["chunks"]

        pairs = [list(range(j, min(j + 2, len(chunks)))) for j in range(0, len(chunks), 2)]

        def tr(pj):
            cis = pairs[pj]
            ti = nxt("pTb", 2)
            c00 = chunks[cis[0]][0]
            nbt = sum(chunks[ci][1] // 128 for ci in cis)
            for bb in range(nbt):
                col = c00 + bb * 128
                S.op("tensor", lambda e, par=par, col=col, ti=ti, bb=bb: e.transpose(
                    out=pTb[ti][:, bb, :], in_=wfull[par][:, col:col + 128], identity=ident_bf[:]),
                    reads=[("wf", par, ci) for ci in cis] + ["ident_bf"], writes=[("pTb", ti)], inc=(bb == nbt - 1))
            b0 = c00 // 128
            S.op("vector", lambda e, ti=ti, b0=b0, nbt=nbt: e.tensor_copy(out=wTfull[:, b0:b0 + nbt, :], in_=pTb[ti][:, 0:nbt, :]),
                 reads=[("pTb", ti)], writes=[("wTfull", ci) for ci in cis])
        tr(0)
        for pj in range(len(pairs)):
            if pj + 1 < len(pairs):
                tr(pj + 1)
            for ci in pairs[pj]:
                u["pv"](ci, chunks[ci][0], chunks[ci][1] // 128)

    def attn_s3_fin(u):
        u["fin"](u["oi"])
        if u.get("post"):
            u["post"]()

    def pr_s1a(u):
        par = u["par"]
        chunks = u["chunks"]
        u["pis"] = []
        for ci, (c0, w) in enumerate(chunks):
            pi = nxt("pA", 4)
            u["pis"].append(pi)
            last = (ci == len(chunks) - 1)
            u["qk"](pi, ci, c0, w, last)
            si = nxt("sp", 3)
            S.op("scalar", lambda e, pi=pi, si=si, w=w: e.activation(
                out=spb[si][:, 0:w], in_=pA[pi][:, 0:w], func=AF.Exp, scale=-1.0),
                reads=[("pA", pi)], writes=[("sp", si)], extra=u.get("extra", ()))
            S.op("scalar", lambda e, si=si, par=par, c0=c0, w=w: e.activation(
                out=nlbS[par][:, 1 + c0:1 + c0 + w], in_=spb[si][:, 0:w], func=AF.Ln, bias=1.0, scale=1.0),
                reads=[("sp", si)], writes=[("nlb", par, ci)])

    def pr_s1b(u):
        par = u["par"]
        nk = u["nk"]
        chunks = u["chunks"]
        for ci, (c0, w) in enumerate(chunks):
            pi = u["pis"][ci]
            rk = [("pA", pi), ("nlb", par, ci)] + ([("nlb", par, ci - 1), ("u", par, ci - 1)] if ci else [("nlb0", par)])
            if ci == 0:
                S.op("vector", lambda e, pi=pi, par=par, w=w: e.tensor_tensor_scan(
                    out=ubuf[par][:, 0:w], data0=pA[pi][:, 0:w], data1=nlbS[par][:, 0:w], initial=0.0,
                    op0=ALU.add, op1=ALU.add), reads=rk, writes=[("u", par, ci)])
            else:
                S.op("vector", lambda e, pi=pi, par=par, c0=c0, w=w: e.tensor_tensor_scan(
                    out=ubuf[par][:, c0:c0 + w], data0=pA[pi][:, 0:w], data1=nlbS[par][:, c0:c0 + w],
                    initial=ubuf[par][:, c0 - 1:c0], op0=ALU.add, op1=ALU.add), reads=rk, writes=[("u", par, ci)])
        lastc = len(chunks) - 1
        allk = [("u", par, ci) for ci in range(len(chunks))] + [("nlb", par, ci) for ci in range(len(chunks))]
        S.op("vector", lambda e, par=par, nk=nk: e.tensor_tensor(
            out=twin[:], in0=ubuf[par][:, nk - 129:nk - 1], in1=nlbS[par][:, nk - 128:nk], op=ALU.add),
            reads=allk, writes=["twin"])
        S.op("vector", lambda e, par=par: e.scalar_tensor_tensor(
            out=tjunk[:], in0=twin[:], scalar=1.0, in1=nident_f[:], op0=ALU.mult, op1=ALU.mult,
            accum_out=negT[:, par:par + 1]),
            reads=["twin", "nident_f"], writes=["tjunk", ("negT", par)])

    def pr_s2(u):
        par = u["par"]
        for ci, (c0, w) in enumerate(u["chunks"]):
            S.op("scalar", lambda e, par=par, c0=c0, w=w: e.activation(
                out=wfull[par][:, c0:c0 + w], in_=ubuf[par][:, c0:c0 + w], func=AF.Exp,
                bias=negT[:, par:par + 1], scale=1.0),
                reads=[("u", par, ci), ("negT", par)], writes=[("wf", par, ci)])

    def run_pipeline(units):
        n = len(units)
        for k_ in range(n + 2):
            if k_ < n:
                units[k_]["s1a"](units[k_])
            if k_ >= 2:
                attn_s3(units[k_ - 2])
            if k_ < n:
                units[k_]["s1b"](units[k_])
            if 1 <= k_ <= n:
                units[k_ - 1]["s2"](units[k_ - 1])
            if k_ >= 2:
                attn_s3_fin(units[k_ - 2])

    def load_cache(s, extra=()):
        bi = s % 2
        S.dma("gpsimd", f"ckb{bi}", ckb[bi][:], ck[s].rearrange("(t p) f -> p t f", p=128),
              writes=[("ckb", bi)], extra=extra)
        S.dma("gpsimd", f"cvb{bi}", cvb[bi][:], cv[s].rearrange("(t p) f -> p t f", p=128),
              writes=[("cvb", bi)], extra=extra)

    def make_prompt_unit(h, i, par):
        nk = 1024 + 128 * (i + 1)
        chunks = [(c0, min(512, nk - c0)) for c0 in range(0, nk, 512)]
        tcols = slice(i * 128, (i + 1) * 128)
        u = {"par": par, "nk": nk, "chunks": chunks, "s1a": pr_s1a, "s1b": pr_s1b, "s2": pr_s2}

        def qk(pi, ci, c0, w, last):
            rk = [("qT", h), ("kT", h)]
            if not last:
                S.op("tensor", lambda e: e.matmul(pA[pi][:, 0:w], lhsT=qT[:, h, tcols], rhs=kT[:, h, c0:c0 + w],
                                                  start=True, stop=True),
                     reads=rk, writes=[("pA", pi)], inc=True)
                return
            wd = w - 128
            if wd > 0:
                S.op("tensor", lambda e: e.matmul(pA[pi][:, 0:wd], lhsT=qT[:, h, tcols], rhs=kT[:, h, c0:c0 + wd],
                                                  start=True, stop=True),
                     reads=rk, writes=[("pA", pi)], inc=False)
            S.op("tensor", lambda e: e.matmul(pA[pi][:, wd:w], lhsT=qT[:, h, tcols], rhs=kT[:, h, c0 + wd:c0 + w],
                                              start=True, stop=False),
                 reads=rk, writes=[("pA", pi)], inc=False)
            S.op("tensor", lambda e: e.matmul(pA[pi][:, wd:w], lhsT=maskTd[:], rhs=ident_bf[:],
                                              start=False, stop=True),
                 reads=["maskTd", "ident_bf"], writes=[("pA", pi)], inc=True)

        state = {"blk": 0}

        def pv(ci, c0, nb):
            oi = u["oi"]
            for b in range(nb):
                blk = c0 // 128 + b
                lastb = (blk == nk // 128 - 1)
                S.op("tensor", lambda e, blk=blk, lastb=lastb: e.matmul(
                    pTf[oi][:, 0, :], lhsT=v_hm[:, h, blk, :], rhs=wTfull[:, blk, :],
                    start=(blk == 0), stop=lastb),
                    reads=[("wTfull", ci), ("v", blk)], writes=[("pTf", oi)], inc=(lastb or b == nb - 1))

        def fin(oi):
            S.op("vector", lambda e: e.tensor_tensor(out=mixsb[:, h, tcols], in0=pTf[oi][:, 0, :],
                                                     in1=gsb[:, h, tcols], op=ALU.mult),
                 reads=[("pTf", oi), ("gsb", h)], writes=[("mixsb", h)])
        u["qk"] = qk; u["pv"] = pv; u["fin"] = fin
        return u

    for i in range(2):
        S.op("gpsimd", lambda e, i=i: e.memset(nlbS[i][:, 0:1], 0.0), writes=[("nlb0", i)], extra=t_proj_done)
    units = []
    n = 0
    for h in range(NH):
        for i in range(8):
            units.append(make_prompt_unit(h, i, n % 2))
            n += 1
    units[0]["extra"] = t_proj_done
    units[1]["extra"] = t_proj_done
    units[4 * 8 - 1]["post"] = (lambda: load_cache(0, extra=S.snapshot()))
    run_pipeline(units)
    t_attn_done = S.snapshot()

    def issue_wo(cc, extra=()):
        S.dma("gpsimd", f"wo{cc % 2}", wo[cc % 2][:].rearrange("p c f -> p (c f)"), w_out[cc],
              writes=[("wo", cc % 2)], extra=extra)

    def make_sample_unit(s, g, par):
        nk = 1152
        chunks = [(0, 512), (512, 512), (1024, 128)]
        bi = s % 2
        qcols = slice(SMP0 + 32 * s, SMP0 + 32 * s + 32)
        scols = slice(32 * s, 32 * s + 32)
        u = {"par": par, "nk": nk, "chunks": chunks, "e": esb, "P": psb, "ek": "es", "pk": "Ps",
             "s1a": attn_s1a, "s1b": attn_s1b, "s2": attn_s2}

        def qk(pi, ci, c0, w, last):
            for j in range(4):
                h = 4 * g + j
                rhs = kTp[:, h, c0:c0 + w] if ci < 2 else kT_s[:, h, :]
                rk = ["q_s", ("kTp", h)] if ci < 2 else ["q_s", ("kT_s", h)]
                S.op("tensor", lambda e, j=j, h=h, rhs=rhs: e.matmul(
                    pA[pi][32 * j:32 * j + 32, 0:w], lhsT=q_s[:, h, scols], rhs=rhs,
                    start=True, stop=(not last), tile_position=(0, 32 * j), skip_group_check=True),
                    reads=rk, writes=[("pA", pi)], inc=(j == 3 and not last))
            if last:
                S.op("tensor", lambda e: e.matmul(pA[pi][:, 0:128], lhsT=maskTs[s][:], rhs=ident_bf[:],
                                                  start=False, stop=True, skip_group_check=True),
                     reads=[("maskTs", s), "ident_bf"], writes=[("pA", pi)], inc=True)

        def pv(ci, c0, nb):
            pass

        def fin(oi):
            for j in range(4):
                h = 4 * g + j
                for blk in range(9):
                    if blk < 8:
                        lhsT = cvb[bi][:, blk, h * 128:(h + 1) * 128]
                        rk = [("cvb", bi)]
                    else:
                        lhsT = v_s[:, h * 128:(h + 1) * 128]
                        rk = [("v_s", h)]
                    S.op("tensor", lambda e, j=j, blk=blk, lhsT=lhsT: e.matmul(
                        pTf[oi][:, 0, 32 * j:32 * j + 32], lhsT=lhsT, rhs=wTfull[:, blk, 32 * j:32 * j + 32],
                        start=(blk == 0), stop=(blk == 8), skip_group_check=True),
                        reads=rk + [("wTfull", 0), ("wTfull", 1), ("wTfull", 2)], writes=[("pTf", oi)],
                        inc=(blk == 8 and j == 3))
            S.op("vector", lambda e: e.tensor_tensor(
                out=mixsb[:, 4 * g:4 * g + 4, qcols],
                in0=pTf[oi][:, 0, :].rearrange("p (j q) -> p j q", j=4),
                in1=g_s[:, 4 * g:4 * g + 4, scols], op=ALU.mult),
                reads=[("pTf", oi), "g_s"],
                writes=[("mixsb", 4 * g + j) for j in range(4)])
        u["qk"] = qk; u["pv"] = pv; u["fin"] = fin
        return u

    def transpose_kcache(s, g):
        bi = s % 2
        ex = t_attn_done if s == 0 else ()
        for h in range(4 * g, 4 * g + 4):
            for half in range(2):
                ti = nxt("pTb", 2)
                for b in range(4):
                    t = half * 4 + b
                    S.op("tensor", lambda e, t=t, b=b, ti=ti, h=h: e.transpose(
                        out=pTb[ti][:, b, :], in_=ckb[bi][:, t, h * 128:(h + 1) * 128], identity=ident_bf[:]),
                        reads=[("ckb", bi), "ident_bf"], writes=[("pTb", ti)], inc=(b == 3))
                dst = kTp[:, h, half * 512:(half + 1) * 512].rearrange("p (b k) -> p b k", b=4)
                S.op("vector", lambda e, dst=dst, ti=ti: e.tensor_copy(out=dst, in_=pTb[ti][:, 0:4, :]),
                     reads=[("pTb", ti)], writes=[("kTp", h)], extra=ex)

    load_cache(1, extra=t_attn_done)
    S.op("gpsimd", lambda e: e.tensor_copy(out=q_s[:], in_=qT[:, :, SMP0:SMP0 + 128]),
         reads=[("qT", h) for h in range(NH)], writes=["q_s"], extra=t_attn_done)
    S.op("gpsimd", lambda e: e.tensor_copy(out=g_s[:], in_=gsb[:, :, SMP0:SMP0 + 128]),
         reads=[("gsb", h) for h in range(NH)], writes=["g_s"], extra=t_attn_done)
    t_r1_free = S.snapshot()
    for i in range(2):
        S.op("gpsimd", lambda e, i=i: e.memset(psb[i][:, 0:1], 0.0), writes=[("Ps", i)], extra=t_attn_done)
    sunits = []
    for s in range(4):
        us = [make_sample_unit(s, g, g) for g in range(2)]
        us[0]["pre"] = (lambda s=s: transpose_kcache(s, 0))
        us[1]["pre"] = (lambda s=s: transpose_kcache(s, 1))
        if s == 0:
            us[1]["post"] = (lambda: load_cache(2))
        elif s == 1:
            us[1]["post"] = (lambda: (load_cache(3), issue_wo(0, extra=t_r1_free), issue_wo(1, extra=t_r1_free)))
        sunits += us
    sunits[0]["extra"] = t_attn_done
    sunits[1]["extra"] = t_attn_done
    run_pipeline(sunits)
    t_samp_done = S.snapshot()

    for tt in range(9):
        r0 = TC + tt * 128
        S.dma("sync", f"xres{tt}", yv(tt), xa[r0:r0 + 128, :], writes=[("yacc", tt)], extra=t_samp_done)
    S.dma("sync", "const2", fgb[:], final_g.partition_broadcast(128), writes=["fgb"], extra=t_samp_done)

    S.group_begin("sync", "yout")

    def tokcols(tt):
        return slice(tt * 128, (tt + 1) * 128) if tt < 8 else slice(SMP0, SMP0 + 128)

    for cc in range(4):
        for tt in range(9):
            pi = nxt("pA", 4)
            tc_ = tokcols(tt)
            for ec in range(16):
                lhsT = mixsb[:, ec, tc_] if ec < 8 else mixcv[:, ec - 8, tc_]
                rk = [("mixsb", ec)] if ec < 8 else [("mixcv", ec - 8)]
                S.op("tensor", lambda e, pi=pi, lhsT=lhsT, ec=ec, cc=cc: e.matmul(
                    pA[pi][:, 0:512], lhsT=lhsT, rhs=wo[cc % 2][:, ec, :], start=(ec == 0), stop=(ec == 15)),
                    reads=rk + [("wo", cc % 2)], writes=[("pA", pi)], inc=(ec == 15),
                    extra=(t_samp_done if (cc == 0 and tt == 0 and ec == 0) else ()))
            S.op("vector", lambda e, pi=pi, tt=tt, cc=cc: e.tensor_tensor(
                out=yv(tt, cc * 512, (cc + 1) * 512), in0=pA[pi][:, 0:512],
                in1=yv(tt, cc * 512, (cc + 1) * 512), op=ALU.add),
                reads=[("pA", pi), ("yacc", tt)], writes=[("yacc", tt)])
            if cc == 3:
                st = 17 + tt
                S.op("scalar", lambda e, tt=tt, st=st: e.activation(
                    out=junk2[:], in_=yv(tt), func=AF.Square, accum_out=stat[:, st:st + 1]),
                    reads=[("yacc", tt)], writes=["junk2", ("ss", st)])
                S.op("scalar", lambda e, st=st: e.activation(
                    out=stat[:, 32 + st:33 + st], in_=stat[:, st:st + 1], func=AF.Ln, scale=1.0 / D, bias=EPS),
                    reads=[("ss", st)], writes=[("lr", st)])
                S.op("scalar", lambda e, st=st: e.activation(
                    out=stat[:, 64 + st:65 + st], in_=stat[:, 32 + st:33 + st], func=AF.Exp, scale=-0.5),
                    reads=[("lr", st)], writes=[("rr", st)])
                S.op("vector", lambda e, tt=tt, st=st: e.scalar_tensor_tensor(
                    out=yv(tt), in0=yv(tt), scalar=stat[:, 64 + st:65 + st], in1=fgb[:],
                    op0=ALU.mult, op1=ALU.mult),
                    reads=[("yacc", tt), ("rr", st), "fgb"], writes=[("yacc", tt)])
                dst = yp[tt * 128:(tt + 1) * 128, :] if tt < 8 else ys[:, :]
                S.dma("sync", "yout", dst, yv(tt), reads=[("yacc", tt)])
        if cc + 2 < 4:
            issue_wo(cc + 2)

    S.group_end("yout")
    return S


_PROGRAM = None


def _get_program():
    global _PROGRAM
    if _PROGRAM is None:
        _PROGRAM = build_program()
    return _PROGRAM


def kernel(x_prompt, x_sample, cache_k, cache_v, state_conv, norm_g, w_in, conv_w, w_out, final_g):
    f = lambda a: np.ascontiguousarray(np.asarray(a, dtype=np.float32))
    x_prompt, x_sample = f(x_prompt), f(x_sample)
    cache_k, cache_v, state_conv = f(cache_k), f(cache_v), f(state_conv)
    norm_g, w_in, conv_w, w_out, final_g = f(norm_g), f(w_in), f(conv_w), f(w_out), f(final_g)
    nc = build_program()
    w3 = w_in[0].reshape(NDC, 128, 8192)
    w_in_r = np.empty((64, 128, NDC * 128), np.float32)
    for ci, f in enumerate(_forder_cols()):
        w_in_r[ci] = w3[:, :, f * 128:(f + 1) * 128].transpose(1, 0, 2).reshape(128, NDC * 128)
    wo3 = w_out[0].reshape(NDC, 128, D)
    w_out_r = np.empty((4, 128, NDC * 512), np.float32)
    for cc in range(4):
        w_out_r[cc] = wo3[:, :, cc * 512:(cc + 1) * 512].transpose(1, 0, 2).reshape(128, NDC * 512)
    in_maps = []
    for c in range(NCORES):
        b, half = c // 2, c % 2
        xa = np.zeros((TC + TO + TS, D), np.float32)
        if half == 1:
            xa[0:TC] = x_prompt[b, 0:1024]
        xa[TC:TC + TO] = x_prompt[b, half * 1024:(half + 1) * 1024]
        xa[TC + TO:] = x_sample[4 * c:4 * c + 4].reshape(TS, D)
        in_maps.append({
            "xa": xa,
            "ck": np.ascontiguousarray(cache_k[0, 4 * c:4 * c + 4].reshape(4, 1024, 1024)),
            "cv": np.ascontiguousarray(cache_v[0, 4 * c:4 * c + 4].reshape(4, 1024, 1024)),
            "sc": np.ascontiguousarray(state_conv[0, 4 * c:4 * c + 4]),
            "norm_g": norm_g[0], "final_g": final_g, "conv_w": conv_w[0],
            "w_in": w_in_r, "w_out": w_out_r,
        })
    res = run_bass_kernel_spmd(nc, in_maps, core_ids=list(range(NCORES)))
    R = res.results
    y_prompt = np.zeros((4, 2048, D), np.float32)
    y_sample = np.zeros((32, 32, D), np.float32)
    k_prompt = np.zeros((1, 4, 2048, 8, 128), np.float32)
    v_prompt = np.zeros((1, 4, 2048, 8, 128), np.float32)
    conv_prompt = np.zeros((1, 4, 2, 1024), np.float32)
    k_sample = np.zeros((1, 32, 32, 8, 128), np.float32)
    v_sample = np.zeros((1, 32, 32, 8, 128), np.float32)
    conv_sample = np.zeros((1, 32, 2, 1024), np.float32)
    for c in range(NCORES):
        b, half = c // 2, c % 2
        r = R[c]
        sl = slice(half * 1024, (half + 1) * 1024)
        y_prompt[b, sl] = r["yp"]
        k_prompt[0, b, sl] = r["kp"].reshape(1024, 8, 128)
        v_prompt[0, b, sl] = r["vp"].reshape(1024, 8, 128)
        if half == 1:
            conv_prompt[0, b] = r["cp"]
        y_sample[4 * c:4 * c + 4] = r["ys"].reshape(4, 32, D)
        k_sample[0, 4 * c:4 * c + 4] = r["ksn"].reshape(4, 32, 8, 128)
        v_sample[0, 4 * c:4 * c + 4] = r["vsn"].reshape(4, 32, 8, 128)
        conv_sample[0, 4 * c:4 * c + 4] = r["cs"]
    return (y_prompt, y_sample, k_prompt, v_prompt, conv_prompt, k_sample, v_sample, conv_sample)
```

```python
import numpy as np
import concourse.bass as bass
import concourse.mybir as mybir
from concourse.bass_utils import run_bass_kernel_spmd

F32 = mybir.dt.float32
BF16 = mybir.dt.bfloat16
AF = mybir.ActivationFunctionType
ALU = mybir.AluOpType

NCORES = 8
D = 2048
NDC = 16
TC = 1024
TO = 1024
TS = 128
HALO0 = TO
SMP0 = TO + 2
TA = TO + 2 + TS
NH = 8
SCALE = 128.0 ** -0.5
NEG = -30000.0
NEGD = -30.0
EPS = 1e-6
SB_BASE = 16640
SB_END = 229376
NWSLOT = 4


class Sched:
    ENG = ["sync", "scalar", "vector", "gpsimd", "tensor"]

    def __init__(self, nc):
        self.nc = nc
        self.ops = {e: [] for e in self.ENG}
        self.sem = {e: nc.alloc_semaphore("prog_" + e) for e in self.ENG}
        self.cnt = {e: 0 for e in self.ENG}
        self.seen = {e: {} for e in self.ENG}
        self.dsem = {}
        self.dcnt = {}
        self.lastw = {}
        self.readers = {}
        self.pending = {e: [] for e in self.ENG}
        self.groups = {}

    def _deps_for(self, reads, writes):
        deps = []
        for k in reads:
            w = self.lastw.get(k)
            if w is not None:
                deps.append(w)
        for k in writes:
            w = self.lastw.get(k)
            if w is not None:
                deps.append(w)
            deps.extend(self.readers.get(k, {}).values())
        return deps

    def _waits(self, eng, deps):
        waits = []
        for d in deps:
            if d is None:
                continue
            if d[1] is None:
                assert d[2] == eng == "tensor", (d, eng)
                continue
            sem, val = d[0], d[1]
            k = sem.name
            if self.seen[eng].get(k, 0) >= val:
                continue
            self.seen[eng][k] = val
            waits.append((sem, val))
        return waits

    def _register(self, tok, reads, writes):
        for k in reads:
            r = self.readers.setdefault(k, {})
            name = tok[0].name if tok[0] is not None else ("pend_" + tok[2])
            r[name] = tok
        for k in writes:
            self.lastw[k] = tok
            self.readers[k] = {}

    def op(self, eng, fn, reads=(), writes=(), inc=True, extra=()):
        deps = self._deps_for(reads, writes) + [x for x in extra if x is not None]
        waits = self._waits(eng, deps)
        if inc:
            self.cnt[eng] += 1
            tok = [self.sem[eng], self.cnt[eng], eng]
            for p in self.pending[eng]:
                p[0], p[1] = tok[0], tok[1]
            self.pending[eng] = []
        else:
            assert eng == "tensor"
            tok = [None, None, eng]
            self.pending[eng].append(tok)
        self._register(tok, reads, writes)
        self.ops[eng].append((fn, waits, tok if inc else None, 1))
        return tok

    def dma(self, eng, key, out, in_, reads=(), writes=(), extra=(), small=False):
        if key not in self.dsem:
            self.dsem[key] = self.nc.alloc_semaphore("dma_" + key)
            self.dcnt[key] = 0
        deps = self._deps_for(reads, writes) + [x for x in extra if x is not None]
        if key not in self.groups and self.dcnt[key] > 0:
            deps.append([self.dsem[key], self.dcnt[key], "dma"])
        waits = self._waits(eng, deps)
        self.dcnt[key] += 16
        tok = [self.dsem[key], self.dcnt[key], "dma"]
        if key in self.groups:
            self.groups[key].append(tok)
        self._register(tok, reads, writes)
        nc = self.nc

        def fn(e):
            if small:
                with nc.allow_non_contiguous_dma(reason="tiny strided transfer"):
                    return e.dma_start(out=out, in_=in_)
            return e.dma_start(out=out, in_=in_)
        self.ops[eng].append((fn, waits, tok, 16))
        return tok

    def group_begin(self, eng, key):
        if key not in self.dsem:
            self.dsem[key] = self.nc.alloc_semaphore("dma_" + key)
            self.dcnt[key] = 0
        assert key not in self.groups
        self.groups[key] = []
        if self.dcnt[key] > 0:
            w = self._waits(eng, [[self.dsem[key], self.dcnt[key], "dma"]])
            if w:
                self.ops[eng].append((lambda e: e.nop(), w, None, 0))

    def group_end(self, key):
        for t in self.groups.pop(key):
            t[1] = self.dcnt[key]

    def snapshot(self):
        toks = []
        assert not self.groups, "open DMA group at snapshot"
        for e in self.ENG:
            assert not self.pending[e], "pending PE tokens at snapshot"
            if self.cnt[e] > 0:
                toks.append([self.sem[e], self.cnt[e], e])
        for k, s in self.dsem.items():
            toks.append([s, self.dcnt[k], "dma"])
        return toks

    def all_dma_tokens(self):
        return [[s, self.dcnt[k], "dma"] for k, s in self.dsem.items()]

    def emit(self, block):
        final = self.all_dma_tokens()

        def mk(ename):
            def body(e):
                for fn, waits, tok, n in self.ops[ename]:
                    for sem, val in waits:
                        e.wait_ge(sem, val)
                    ins = fn(e)
                    if tok is not None:
                        ins.then_inc(tok[0], n)
                if ename == "sync":
                    for sem, val, _ in final:
                        e.wait_ge(sem, val)
            return body
        block.sync(mk("sync"))
        block.scalar(mk("scalar"))
        block.vector(mk("vector"))
        block.gpsimd(mk("gpsimd"))
        block.tensor(mk("tensor"))


class Arena:
    def __init__(self, nc):
        self.nc = nc
        self.n = 0

    def at(self, name, shape, dt, off):
        sz = int(np.prod(shape[1:])) * (4 if dt == F32 else 2)
        assert off % 32 == 0, (name, off)
        assert SB_BASE <= off and off + sz <= SB_END, (name, off, sz)
        self.n += 1
        return self.nc.alloc_sbuf_tensor_at(f"{name}_{self.n}", list(shape), dt, offset=off)


def _forder():
    fo = []
    for j in range(8):
        fo += [("c", j, 40 + j), ("u", j, 48 + j), ("b", j, 32 + j), ("gcv", j, 56 + j)]
        fo += [("k", j, 8 + j)]
    fo += [("v", h, 16 + h) for h in range(8)]
    fo += [("q", h, h) for h in range(8)]
    fo += [("gsb", h, 24 + h) for h in range(8)]
    return fo


def _forder_cols():
    return [f for (_, _, f) in _forder()]


def build_program():
    nc = bass.Bass("TRN2", target_bir_lowering=False)
    S = _build_body(nc)
    with nc.Block() as block:
        S.emit(block)
    return nc


def _build_body(nc):
    S = Sched(nc)
    A = Arena(nc)

    def din(name, shape):
        return nc.dram_tensor(name, list(shape), F32, kind="ExternalInput").ap()

    def dout(name, shape):
        return nc.dram_tensor(name, list(shape), F32, kind="ExternalOutput").ap()

    xa = din("xa", [TC + TO + TS, D])
    ck = din("ck", [4, 1024, 1024])
    cv = din("cv", [4, 1024, 1024])
    sc = din("sc", [4, 2, 1024])
    norm_g = din("norm_g", [D])
    final_g = din("final_g", [D])
    conv_w = din("conv_w", [3, 1024])
    w_in = din("w_in", [64, 128, NDC * 128])
    w_out = din("w_out", [4, 128, NDC * 512])
    yp = dout("yp", [TO, D])
    ys = dout("ys", [TS, D])
    kp = dout("kp", [TO, 1024])
    vp = dout("vp", [TO, 1024])
    cp = dout("cp", [2, 1024])
    ksn = dout("ksn", [TS, 1024])
    vsn = dout("vsn", [TS, 1024])
    cs = dout("cs", [4, 2, 1024])

    o = SB_BASE
    O_CONST = o; o += 6144
    O_R1 = o; o += 36928
    O_VT = o; o += 32768
    O_KT = o; o += 32800
    O_KTS = o; o += 4096
    O_MCV = o; o += 18464
    O_R2A = o; o += 18464
    O_R2B = o; o += 18464
    O_WR = o; o += NWSLOT * 4096
    O_WST = o; o += 2 * 8192
    O_TR = o
    assert O_TR + 10240 + 512 <= SB_END, O_TR
    O_ATT = O_R2B
    ATT_SZ = SB_END - O_ATT

    c = O_CONST
    ident_bf = A.at("ident_bf", [128, 128], BF16, c); c += 256
    ident_f = A.at("ident_f", [128, 128], F32, c); c += 512
    maskTd = A.at("maskTd", [128, 128], BF16, c); c += 256
    maskTs = []
    for s in range(4):
        maskTs.append(A.at(f"maskTs{s}", [128, 128], BF16, c)); c += 256
    ones = A.at("ones", [128, 512], F32, c); c += 2048
    convw = A.at("convw", [128, 8, 3], F32, c); c += 128
    scT = A.at("scT", [128, 8, 4, 2], F32, c); c += 256
    stat = A.at("stat", [128, 96], F32, c); c += 384
    negT = A.at("negT", [128, 2], F32, c); c += 32
    mtmp = A.at("mtmp", [128, 128], F32, c); c += 512
    nident_f = A.at("nident_f", [128, 128], F32, c); c += 512
    assert c <= O_CONST + 6144

    hTC = A.at("hTC", [128, NDC, TC], BF16, O_R1)
    qT = A.at("qT", [128, NH, TA], BF16, O_R1)
    gsb = A.at("gsb", [128, NH, TA], BF16, O_R1 + 18464)
    hTA = A.at("hTA", [128, NDC, TA], BF16, O_R2A)
    mixsb = A.at("mixsb", [128, NH, TA], BF16, O_R2A)
    v_hm = A.at("v_hm", [128, NH, 16, 128], BF16, O_VT)
    kT = A.at("kT", [128, NH, 2050], BF16, O_KT)
    kT_s = A.at("kT_s", [128, NH, 128], BF16, O_KTS)
    v_s = A.at("v_s", [128, 1024], BF16, O_KTS + 2048)
    mixcv = A.at("mixcv", [128, NH, TA], BF16, O_MCV)
    wring = [A.at(f"wr{i}", [128, NDC, 128], BF16, O_WR + 4096 * i) for i in range(NWSLOT)]
    wst = [A.at(f"wst{i}", [128, NDC, 128], F32, O_WST + 8192 * i) for i in range(2)]
    kfm = [A.at(f"kfm{i}", [128, 512], F32, O_TR + 2048 * i) for i in range(2)]
    ktm = [A.at(f"ktm{i}", [128, 4, 128], F32, O_TR + 4096 + 2048 * i) for i in range(3)]
    xt = [A.at(f"xt{i}", [128, D], F32, O_KT + 8192 * i) for i in range(2)]
    xt.append(A.at("xt2", [128, D], F32, O_MCV))
    xs = [A.at(f"xs{i}", [128, D], BF16, O_KT + 16384 + 4096 * i) for i in range(2)]
    junk = A.at("junk", [128, D], BF16, O_KT + 24576)
    gbc = A.at("gbc", [128, D], F32, O_KT + 28672)
    assert 28672 + 8192 <= 32800 + 4096
    c_sb = A.at("c_sb", [128, TA], F32, O_VT)
    cu_p = A.at("cu_p", [128, 1026], F32, O_VT + 4640)
    cu_s = A.at("cu_s", [128, 4, 34], F32, O_VT + 4640 + 4128)
    acc = A.at("acc", [128, 1152], F32, O_VT + 4640 + 4128 + 576)
    sg = A.at("sg", [128, TA], F32, O_VT + 4640 + 4128 + 576 + 4608)
    b_sb = A.at("b_sb", [128, TA], F32, O_VT + 4640 + 4128 + 576 + 4608 + 4640)
    a = O_ATT
    spb = []; gb_ = []; wfull = []
    for i in range(3):
        spb.append(A.at(f"sp{i}", [128, 512], F32, a)); a += 2048
    for i in range(4):
        gb_.append(A.at(f"g{i}", [128, 512], BF16, a)); a += 1024
    for i in range(2):
        wfull.append(A.at(f"wf{i}", [128, 2048], BF16, a)); a += 4096
    wTfull = A.at("wTfull", [128, 17, 128], BF16, a)
    g_s = A.at("g_s", [128, NH, 128], BF16, a + 2304)
    a += 4352
    a_ph = a
    nlbS = []; ubuf = []
    for i in range(2):
        ubuf.append(A.at(f"u{i}", [128, 2048], F32, a)); a += 8192
        nlbS.append(A.at(f"nlbS{i}", [128, 2056], F32, a)); a += 8224
    twin = A.at("twin", [128, 128], F32, a); a += 512
    tjunk = A.at("tjunk", [128, 128], F32, a); a += 512
    assert a <= SB_END, a
    a = a_ph
    esb = []; psb = []
    for i in range(2):
        esb.append(A.at(f"es{i}", [128, 1152], F32, a)); a += 4608
        psb.append(A.at(f"Ps{i}", [128, 1160], F32, a)); a += 4640
    kTp = A.at("kTp", [128, NH, 1024], BF16, a); a += 16384
    q_s = A.at("q_s", [128, NH, 128], BF16, a); a += 2048
    assert a <= SB_END, a
    ckb = [A.at(f"ckb{i}", [128, 8, 1024], BF16, O_VT + 16384 * i) for i in range(2)]
    cvb = [A.at(f"cvb{i}", [128, 8, 1024], BF16, O_VT + 32768 + 16384 * i) for i in range(2)]
    assert O_VT + 65536 <= O_KTS
    wo = [A.at(f"wo{i}", [128, NDC, 512], BF16, O_R1 + 16384 * i) for i in range(2)]
    yacc8 = A.at("yacc8", [128, 8, D], F32, O_VT)
    assert O_VT + 65536 <= O_MCV
    fgb = A.at("fgb", [128, D], F32, O_ATT)
    yacc_s = A.at("yacc_s", [128, D], F32, O_ATT + 12288)

    def yv(tt, lo=0, hi=D):
        return yacc8[:, tt, lo:hi] if tt < 8 else yacc_s[:, lo:hi]
    junk2 = A.at("junk2", [128, D], BF16, O_ATT + 8192)

    pA = [nc.alloc_psum_tensor(f"pA{i}", [128, 512], F32) for i in range(4)]
    pTf = [nc.alloc_psum_tensor(f"pTf{i}", [128, 4, 128], F32) for i in range(2)]
    pTb = [nc.alloc_psum_tensor(f"pTb{i}", [128, 8, 128], BF16) for i in range(2)]

    ring = {"pA": 0, "pTf": 0, "pTb": 0, "w": 0, "kfm": 0, "ktm": 0, "sp": 0, "g": 0, "x": 0, "xs": 0}

    def nxt(name, n):
        i = ring[name] % n
        ring[name] += 1
        return i

    S.op("gpsimd", lambda e: e.memset(mtmp[:], 0.0), writes=["mtmp"])
    S.op("gpsimd", lambda e: e.affine_select(out=ident_f[:], in_=mtmp[:], pattern=[[-1, 128]],
                                             compare_op=ALU.not_equal, fill=1.0, base=0,
                                             channel_multiplier=1), reads=["mtmp"], writes=["ident_f"])
    S.op("gpsimd", lambda e: e.tensor_copy(out=ident_bf[:], in_=ident_f[:]), reads=["ident_f"], writes=["ident_bf"])
    S.op("gpsimd", lambda e: e.affine_select(out=maskTd[:], in_=mtmp[:], pattern=[[1, 128]],
                                             compare_op=ALU.is_gt, fill=NEGD, base=0,
                                             channel_multiplier=-1), reads=["mtmp"], writes=["maskTd"])
    S.op("gpsimd", lambda e: e.affine_select(out=nident_f[:], in_=mtmp[:], pattern=[[-1, 128]],
                                             compare_op=ALU.not_equal, fill=-1.0, base=0,
                                             channel_multiplier=1), reads=["mtmp"], writes=["nident_f"])
    S.op("gpsimd", lambda e: e.memset(ones[:], 1.0), writes=["ones"])

    msk_tmp = A.at("msk_tmp", [128, 128], F32, O_TR + 10240)
    for s in range(4):
        S.op("gpsimd", lambda e, s=s: e.affine_select(out=msk_tmp[:], in_=mtmp[:], pattern=[[0, 4], [0, 32]],
                                                      compare_op=ALU.is_ge, fill=NEG, base=-32 * s,
                                                      channel_multiplier=1),
             reads=["mtmp", ("maskTs", s - 1)], writes=["msk_tmp"])
        S.op("gpsimd", lambda e, s=s: e.affine_select(out=msk_tmp[:], in_=msk_tmp[:], pattern=[[0, 4], [1, 32]],
                                                      compare_op=ALU.is_gt, fill=NEG, base=32 * s,
                                                      channel_multiplier=-1),
             reads=["msk_tmp"], writes=["msk_tmp"])
        S.op("gpsimd", lambda e, s=s: e.tensor_copy(out=maskTs[s][:], in_=msk_tmp[:]),
             reads=["msk_tmp"], writes=[("maskTs", s)])
    S.dma("sync", "gbc", gbc[:], norm_g.partition_broadcast(128), writes=["gbc"])
    S.group_begin("gpsimd", "const")
    for j in range(8):
        S.dma("gpsimd", "const", convw[:, j, :], conv_w[:, j * 128:(j + 1) * 128].rearrange("i p -> p i"),
              writes=[("convw", j)], small=True)
        for s in range(4):
            S.dma("gpsimd", "const", scT[:, j, s, :], sc[s, :, j * 128:(j + 1) * 128].rearrange("r p -> p r"),
                  writes=[("scT", j, s)], small=True)
    S.group_end("const")


    forder = _forder()
    wload_state = {"next": 0, "pending_cast": []}

    def issue_wload(extra=()):
        i = wload_state["next"]
        if i >= len(forder):
            return
        wload_state["next"] += 1
        slot = i % NWSLOT
        sti = i % 2
        S.dma("sync", f"ws{sti}", wst[sti][:].rearrange("p c f -> p (c f)"), w_in[i], writes=[("wst", sti)], extra=extra)
        wload_state["pending_cast"].append((slot, sti))

    def issue_wcast():
        if not wload_state["pending_cast"]:
            return
        slot, sti = wload_state["pending_cast"].pop(0)
        S.op("vector", lambda e, slot=slot, sti=sti: e.tensor_copy(out=wring[slot][:], in_=wst[sti][:]),
             reads=[("wst", sti)], writes=[("w", slot)])

    issue_wload(); issue_wload()
    issue_wcast(); issue_wcast()
    issue_wload(); issue_wload()

    tile_order = [7] + list(range(8, 17)) + list(range(0, 7))
    for n_i, tt in enumerate(tile_order):
        xi = nxt("x", 3)
        S.dma("sync", f"x{xi}", xt[xi][:], xa[tt * 128:(tt + 1) * 128, :], writes=[("xt", xi)])
        S.op("scalar", lambda e, xi=xi, tt=tt: e.activation(out=junk[:], in_=xt[xi][:], func=AF.Square,
                                                            accum_out=stat[:, tt:tt + 1]),
             reads=[("xt", xi)], writes=["junk", ("ss", tt)])
        S.op("scalar", lambda e, tt=tt: e.activation(out=stat[:, 32 + tt:33 + tt], in_=stat[:, tt:tt + 1],
                                                     func=AF.Ln, scale=1.0 / D, bias=EPS),
             reads=[("ss", tt)], writes=[("lr", tt)])
        S.op("scalar", lambda e, tt=tt: e.activation(out=stat[:, 64 + tt:65 + tt], in_=stat[:, 32 + tt:33 + tt],
                                                     func=AF.Exp, scale=-0.5),
             reads=[("lr", tt)], writes=[("rr", tt)])
        si = nxt("xs", 2)
        S.op("vector", lambda e, xi=xi, si=si, tt=tt: e.scalar_tensor_tensor(
            out=xs[si][:], in0=xt[xi][:], scalar=stat[:, 64 + tt:65 + tt], in1=gbc[:],
            op0=ALU.mult, op1=ALU.mult),
            reads=[("xt", xi), ("rr", tt), "gbc"], writes=[("xs", si)])
        for grp in range(4):
            pi = nxt("pTb", 2)
            for jj in range(4):
                dc = grp * 4 + jj
                S.op("tensor", lambda e, si=si, dc=dc, pi=pi, jj=jj: e.transpose(
                    out=pTb[pi][:, jj, :], in_=xs[si][:, dc * 128:(dc + 1) * 128], identity=ident_bf[:]),
                    reads=[("xs", si), "ident_bf"], writes=[("pTb", pi)], inc=(jj == 3))
            if tt < 8:
                dst = hTC[:, grp * 4:(grp + 1) * 4, tt * 128:(tt + 1) * 128]
                wkey = ("hTC", tt)
            elif tt < 16:
                dst = hTA[:, grp * 4:(grp + 1) * 4, (tt - 8) * 128:(tt - 7) * 128]
                wkey = ("hTA", tt - 8)
            else:
                dst = hTA[:, grp * 4:(grp + 1) * 4, SMP0:SMP0 + 128]
                wkey = ("hTA", 8)
            ceng = "vector"
            if ceng == "vector":
                S.op("vector", lambda e, dst=dst, pi=pi: e.tensor_copy(out=dst, in_=pTb[pi][:, 0:4, :]),
                     reads=[("pTb", pi)], writes=[wkey])
            else:
                S.op("scalar", lambda e, dst=dst, pi=pi: e.activation(out=dst, in_=pTb[pi][:, 0:4, :], func=AF.Copy),
                     reads=[("pTb", pi)], writes=[wkey])
            if tt == 7:
                hdst = hTA[:, grp * 4:(grp + 1) * 4, HALO0:HALO0 + 2]
                S.op("vector", lambda e, hdst=hdst, pi=pi: e.tensor_copy(out=hdst, in_=pTb[pi][:, 0:4, 126:128]),
                     reads=[("pTb", pi)], writes=[("hTA", 8)])
    t_norm_done = S.snapshot()

    def hkeys(buf, c0, w):
        if buf == "A":
            lo = min(c0 // 128, 8); hi = min((c0 + w - 1) // 128, 8)
            return [("hTA", t) for t in range(lo, hi + 1)]
        lo = c0 // 128; hi = (c0 + w - 1) // 128
        return [("hTC", t) for t in range(lo, hi + 1)]

    A_CHUNKS_KV = [(0, 512), (512, 512), (1024, TA - 1024)]
    A_CHUNKS_BAL = [(0, 385), (385, 385), (770, TA - 770)]
    C_CHUNKS = [(0, 512), (512, 512)]

    defer_q = []

    def proj_tile(slot, buf, c0, w):
        pi = nxt("pA", 4)
        src = hTA if buf == "A" else hTC
        rk = hkeys(buf, c0, w) + [("w", slot)]
        for dc in range(NDC):
            S.op("tensor", lambda e, pi=pi, slot=slot, dc=dc, src=src, c0=c0, w=w: e.matmul(
                pA[pi][:, 0:w], lhsT=wring[slot][:, dc, :], rhs=src[:, dc, c0:c0 + w],
                start=(dc == 0), stop=(dc == NDC - 1)),
                reads=rk, writes=[("pA", pi)], inc=(dc == NDC - 1))
        while defer_q:
            defer_q.pop(0)()
        return pi

    for ci, (kind, idx, f) in enumerate(forder):
        slot = ci % NWSLOT
        issue_wcast()
        issue_wload()
        extra_first = ()
        if kind == "k" and idx == 0:
            extra_first = t_norm_done
        if kind == "v" and idx == 0:
            extra_first = S.snapshot()
        if kind == "q" and idx == 0:
            extra_first = S.snapshot()
        if kind in ("k", "v"):
            for (c0, w) in C_CHUNKS:
                pi = proj_tile(slot, "C", c0, w)
                if kind == "k":
                    S.op("scalar", lambda e, pi=pi, idx=idx, c0=c0, w=w: e.activation(
                        out=kT[:, idx, c0:c0 + w], in_=pA[pi][:, 0:w], func=AF.Copy),
                        reads=[("pA", pi)], writes=[("kT", idx)], extra=extra_first)
                    extra_first = ()
                else:
                    ki = nxt("kfm", 2)
                    S.op("vector", lambda e, pi=pi, ki=ki, w=w: e.tensor_copy(out=kfm[ki][:, 0:w], in_=pA[pi][:, 0:w]),
                         reads=[("pA", pi)], writes=[("kfm", ki)])
                    def post_ctx(ki=ki, c0=c0, idx=idx, ex=extra_first):
                        ti = nxt("pTf", 2)
                        for b in range(4):
                            S.op("tensor", lambda e, ki=ki, ti=ti, b=b: e.transpose(
                                out=pTf[ti][:, b, :], in_=kfm[ki][:, b * 128:(b + 1) * 128], identity=ident_f[:]),
                                reads=[("kfm", ki), "ident_f"], writes=[("pTf", ti)], inc=(b == 3))
                        t0 = c0 // 128
                        S.op("vector", lambda e, ti=ti, t0=t0, idx=idx: e.tensor_copy(
                            out=v_hm[:, idx, t0:t0 + 4, :], in_=pTf[ti][:, 0:4, :]),
                            reads=[("pTf", ti)], writes=[("v", t) for t in range(t0, t0 + 4)], extra=ex)
                    defer_q.append(post_ctx)
                    extra_first = ()
        A_CHUNKS = A_CHUNKS_KV if kind in ("k", "v") else A_CHUNKS_BAL
        for cc, (c0, w) in enumerate(A_CHUNKS):
            pi = proj_tile(slot, "A", c0, w)
            own_len = max(0, min(c0 + w, TO) - c0)
            has_tail = (c0 + w == TA)
            if kind == "c":
                S.op("scalar", lambda e, pi=pi, c0=c0, w=w: e.activation(
                    out=c_sb[:, c0:c0 + w], in_=pA[pi][:, 0:w], func=AF.Copy),
                    reads=[("pA", pi)], writes=[("c_sb", cc)])
            elif kind == "u":
                S.op("vector", lambda e, pi=pi, c0=c0, n_=own_len: e.tensor_tensor(
                    out=cu_p[:, 2 + c0:2 + c0 + n_], in0=pA[pi][:, 0:n_], in1=c_sb[:, c0:c0 + n_], op=ALU.mult),
                    reads=[("pA", pi), ("c_sb", cc)], writes=[("cu", cc)])
                if has_tail:
                    ho = HALO0 - c0
                    so = SMP0 - c0
                    S.op("vector", lambda e, pi=pi, ho=ho: e.tensor_tensor(
                        out=cu_p[:, 0:2], in0=pA[pi][:, ho:ho + 2], in1=c_sb[:, HALO0:HALO0 + 2], op=ALU.mult),
                        reads=[("pA", pi), ("c_sb", 2)], writes=[("cu", 5)])
                    S.op("gpsimd", lambda e, idx=idx: e.tensor_copy(out=cu_s[:, :, 0:2], in_=scT[:, idx, :, :]),
                         reads=[("scT", idx, s_) for s_ in range(4)], writes=[("cu", 3)])
                    S.op("vector", lambda e, pi=pi, so=so: e.tensor_tensor(
                        out=cu_s[:, :, 2:34], in0=pA[pi][:, so:so + 128].rearrange("p (s t) -> p s t", s=4),
                        in1=c_sb[:, SMP0:SMP0 + 128].rearrange("p (s t) -> p s t", s=4), op=ALU.mult),
                        reads=[("pA", pi), ("c_sb", 2)], writes=[("cu", 4)])
                    S.group_begin("gpsimd", "cout")
                    S.dma("gpsimd", "cout", cp[:, idx * 128:(idx + 1) * 128].rearrange("r p -> p r"),
                          cu_p[:, 1024:1026], reads=[("cu", 2)], small=True)
                    for s in range(4):
                        S.dma("gpsimd", "cout", cs[s, :, idx * 128:(idx + 1) * 128].rearrange("r p -> p r"),
                              cu_s[:, s, 32:34], reads=[("cu", 4)], small=True)
                    S.group_end("cout")
                    allcu = [("cu", i) for i in range(6)]
                    S.op("vector", lambda e, idx=idx: e.tensor_scalar(
                        out=acc[:, 0:1024], in0=cu_p[:, 2:1026], scalar1=convw[:, idx, 2:3], scalar2=None,
                        op0=ALU.mult), reads=allcu + [("convw", idx)], writes=["acc"])
                    S.op("vector", lambda e, idx=idx: e.scalar_tensor_tensor(
                        out=acc[:, 0:1024], in0=cu_p[:, 1:1025], scalar=convw[:, idx, 1:2], in1=acc[:, 0:1024],
                        op0=ALU.mult, op1=ALU.add), reads=allcu + ["acc"], writes=["acc"])
                    S.op("vector", lambda e, idx=idx: e.scalar_tensor_tensor(
                        out=acc[:, 0:1024], in0=cu_p[:, 0:1024], scalar=convw[:, idx, 0:1], in1=acc[:, 0:1024],
                        op0=ALU.mult, op1=ALU.add), reads=allcu + ["acc"], writes=["acc"])
                    accs = acc[:, 1024:1152].rearrange("p (s t) -> p s t", s=4)
                    S.op("vector", lambda e, idx=idx, accs=accs: e.tensor_scalar(
                        out=accs, in0=cu_s[:, :, 2:34], scalar1=convw[:, idx, 2:3], scalar2=None,
                        op0=ALU.mult), reads=allcu + ["acc"], writes=["acc"])
                    S.op("vector", lambda e, idx=idx, accs=accs: e.scalar_tensor_tensor(
                        out=accs, in0=cu_s[:, :, 1:33], scalar=convw[:, idx, 1:2], in1=accs,
                        op0=ALU.mult, op1=ALU.add), reads=allcu + ["acc"], writes=["acc"])
                    S.op("vector", lambda e, idx=idx, accs=accs: e.scalar_tensor_tensor(
                        out=accs, in0=cu_s[:, :, 0:32], scalar=convw[:, idx, 0:1], in1=accs,
                        op0=ALU.mult, op1=ALU.add), reads=allcu + ["acc"], writes=["acc"])
            elif kind == "b":
                S.op("scalar", lambda e, pi=pi, c0=c0, w=w: e.activation(
                    out=b_sb[:, c0:c0 + w], in_=pA[pi][:, 0:w], func=AF.Copy),
                    reads=[("pA", pi)], writes=[("b_sb", cc)])
                if has_tail:
                    S.op("gpsimd", lambda e: e.tensor_tensor(
                        out=acc[:, 0:1024], in0=acc[:, 0:1024], in1=b_sb[:, 0:1024], op=ALU.mult),
                        reads=["acc", ("b_sb", 0), ("b_sb", 1), ("b_sb", 2)], writes=["acc"])
                    S.op("gpsimd", lambda e: e.tensor_tensor(
                        out=acc[:, 1024:1152], in0=acc[:, 1024:1152], in1=b_sb[:, SMP0:SMP0 + 128], op=ALU.mult),
                        reads=["acc", ("b_sb", 2)], writes=["acc"])
            elif kind == "gcv":
                S.op("scalar", lambda e, pi=pi, c0=c0, w=w: e.activation(
                    out=sg[:, c0:c0 + w], in_=pA[pi][:, 0:w], func=AF.Silu),
                    reads=[("pA", pi)], writes=[("sg", cc)])
                if has_tail:
                    S.op("gpsimd", lambda e, idx=idx: e.tensor_tensor(
                        out=mixcv[:, idx, 0:1024], in0=acc[:, 0:1024], in1=sg[:, 0:1024], op=ALU.mult),
                        reads=["acc", ("sg", 0), ("sg", 1), ("sg", 2)], writes=[("mixcv", idx)],
                        extra=(t_norm_done if idx == 0 else ()))
                    S.op("gpsimd", lambda e, idx=idx: e.tensor_tensor(
                        out=mixcv[:, idx, SMP0:SMP0 + 128], in0=acc[:, 1024:1152], in1=sg[:, SMP0:SMP0 + 128], op=ALU.mult),
                        reads=["acc", ("sg", 2)], writes=[("mixcv", idx)])
            elif kind == "q":
                S.op("scalar", lambda e, pi=pi, idx=idx, c0=c0, w=w: e.activation(
                    out=qT[:, idx, c0:c0 + w], in_=pA[pi][:, 0:w], func=AF.Copy, scale=SCALE),
                    reads=[("pA", pi)], writes=[("qT", idx)], extra=extra_first)
                extra_first = ()
            elif kind == "gsb":
                S.op("scalar", lambda e, pi=pi, idx=idx, c0=c0, w=w: e.activation(
                    out=gsb[:, idx, c0:c0 + w], in_=pA[pi][:, 0:w], func=AF.Silu),
                    reads=[("pA", pi)], writes=[("gsb", idx)])
            elif kind in ("k", "v"):
                nb = 4 if cc < 2 else 1
                pc0 = 0 if cc < 2 else 2
                pw = 512 if cc < 2 else 128
                if kind == "k":
                    if cc < 2:
                        S.op("scalar", lambda e, pi=pi, idx=idx, c0=c0: e.activation(
                            out=kT[:, idx, 1024 + c0:1024 + c0 + 512], in_=pA[pi][:, 0:512], func=AF.Copy),
                            reads=[("pA", pi)], writes=[("kT", idx)])
                    else:
                        S.op("scalar", lambda e, pi=pi, idx=idx: e.activation(
                            out=kT_s[:, idx, :], in_=pA[pi][:, 2:130], func=AF.Copy),
                            reads=[("pA", pi)], writes=[("kT_s", idx)])
                ki = nxt("kfm", 2)
                S.op("scalar", lambda e, pi=pi, ki=ki, pc0=pc0, pw=pw: e.activation(
                    out=kfm[ki][:, 0:pw], in_=pA[pi][:, pc0:pc0 + pw], func=AF.Copy),
                    reads=[("pA", pi)], writes=[("kfm", ki)])
                def post_own(ki=ki, nb=nb, kind=kind, cc=cc, c0=c0, idx=idx):
                    ti = nxt("pTf", 2)
                    for b in range(nb):
                        S.op("tensor", lambda e, ki=ki, ti=ti, b=b: e.transpose(
                            out=pTf[ti][:, b, :], in_=kfm[ki][:, b * 128:(b + 1) * 128], identity=ident_f[:]),
                            reads=[("kfm", ki), "ident_f"], writes=[("pTf", ti)], inc=(b == nb - 1))
                    if kind == "v":
                        if cc < 2:
                            t0 = 8 + c0 // 128
                            S.op("vector", lambda e, ti=ti, t0=t0, idx=idx: e.tensor_copy(
                                out=v_hm[:, idx, t0:t0 + 4, :], in_=pTf[ti][:, 0:4, :]),
                                reads=[("pTf", ti)], writes=[("v", t) for t in range(t0, t0 + 4)])
                        else:
                            S.op("vector", lambda e, ti=ti, idx=idx: e.tensor_copy(
                                out=v_s[:, idx * 128:(idx + 1) * 128], in_=pTf[ti][:, 0, :]),
                                reads=[("pTf", ti)], writes=[("v_s", idx)])
                    mi = nxt("ktm", 3)
                    S.op("vector", lambda e, ti=ti, mi=mi, nb=nb: e.tensor_copy(
                        out=ktm[mi][:, 0:nb, :], in_=pTf[ti][:, 0:nb, :]),
                        reads=[("pTf", ti)], writes=[("ktm", mi)])
                    if cc < 2:
                        dst = (kp if kind == "k" else vp)[c0:c0 + 512, idx * 128:(idx + 1) * 128].rearrange(
                            "(b p) f -> p b f", p=128)
                        S.dma("gpsimd", f"ktm{mi}", dst, ktm[mi][:], reads=[("ktm", mi)])
                    else:
                        dst = (ksn if kind == "k" else vsn)[:, idx * 128:(idx + 1) * 128]
                        S.dma("gpsimd", f"ktm{mi}", dst, ktm[mi][:, 0, :], reads=[("ktm", mi)])
                defer_q.append(post_own)

    while defer_q:
        defer_q.pop(0)()
    t_proj_done = S.snapshot()

    def attn_s1a(u):
        par = u["par"]
        ebuf, ek = u["e"], u["ek"]
        chunks = u["chunks"]
        if u.get("pre"):
            u["pre"]()
        for ci, (c0, w) in enumerate(chunks):
            pi = nxt("pA", 4)
            last = (ci == len(chunks) - 1)
            u["qk"](pi, ci, c0, w, last)
            S.op("scalar", lambda e, pi=pi, par=par, c0=c0, w=w: e.activation(
                out=ebuf[par][:, c0:c0 + w], in_=pA[pi][:, 0:w], func=AF.Exp, scale=1.0),
                reads=[("pA", pi)], writes=[(ek, par, ci)], extra=u.get("extra", ()))
        u["spi"] = []
        for ci, (c0, w) in enumerate(chunks):
            si = nxt("sp", 3)
            u["spi"].append(si)
            S.op("scalar", lambda e, si=si, par=par, c0=c0, w=w: e.activation(
                out=spb[si][:, 0:w], in_=ebuf[par][:, c0:c0 + w], func=AF.Ln, bias=1.0, scale=1.0),
                reads=[(ek, par, ci)], writes=[("sp", si)])

    def attn_s1b(u):
        par = u["par"]
        pbuf, pk = u["P"], u["pk"]
        for ci, (c0, w) in enumerate(u["chunks"]):
            si = u["spi"][ci]
            if ci == 0:
                S.op("vector", lambda e, si=si, par=par, w=w: e.tensor_tensor_scan(
                    out=pbuf[par][:, 1:1 + w], data0=ones[:, 0:w], data1=spb[si][:, 0:w], initial=0.0,
                    op0=ALU.mult, op1=ALU.add),
                    reads=[("sp", si), "ones"], writes=[(pk, par)])
            else:
                S.op("vector", lambda e, si=si, par=par, c0=c0, w=w: e.tensor_tensor_scan(
                    out=pbuf[par][:, 1 + c0:1 + c0 + w], data0=ones[:, 0:w], data1=spb[si][:, 0:w],
                    initial=pbuf[par][:, c0:c0 + 1], op0=ALU.mult, op1=ALU.add),
                    reads=[("sp", si), "ones", (pk, par)], writes=[(pk, par)])

    def attn_s2(u):
        par = u["par"]
        ebuf, pbuf, ek, pk = u["e"], u["P"], u["ek"], u["pk"]
        nk = u["nk"]
        S.op("scalar", lambda e, par=par, nk=nk: e.activation(
            out=negT[:, par:par + 1], in_=pbuf[par][:, nk:nk + 1], func=AF.Copy, scale=-1.0),
            reads=[(pk, par)], writes=[("negT", par)])
        for ci, (c0, w) in enumerate(u["chunks"]):
            gi = nxt("g", 4)
            S.op("scalar", lambda e, gi=gi, par=par, c0=c0, w=w: e.activation(
                out=gb_[gi][:, 0:w], in_=pbuf[par][:, c0:c0 + w], func=AF.Exp, bias=negT[:, par:par + 1], scale=1.0),
                reads=[(pk, par), ("negT", par)], writes=[("g", gi)])
            S.op("vector", lambda e, gi=gi, par=par, c0=c0, w=w: e.tensor_tensor(
                out=wfull[par][:, c0:c0 + w], in0=ebuf[par][:, c0:c0 + w], in1=gb_[gi][:, 0:w], op=ALU.mult),
                reads=[(ek, par, ci), ("g", gi)], writes=[("wf", par, ci)])

    def attn_s3(u):
        par = u["par"]
        oi = nxt("pTf", 2)
        u["oi"] = oi
        chunks = u["chunks"]

        pairs = [list(range(j, min(j + 2, len(chunks)))) for j in range(0, len(chunks), 2)]

        def tr(pj):
            cis = pairs[pj]
            ti = nxt("pTb", 2)
            c00 = chunks[cis[0]][0]
            nbt = sum(chunks[ci][1] // 128 for ci in cis)
            for bb in range(nbt):
                col = c00 + bb * 128
                S.op("tensor", lambda e, par=par, col=col, ti=ti, bb=bb: e.transpose(
                    out=pTb[ti][:, bb, :], in_=wfull[par][:, col:col + 128], identity=ident_bf[:]),
                    reads=[("wf", par, ci) for ci in cis] + ["ident_bf"], writes=[("pTb", ti)], inc=(bb == nbt - 1))
            b0 = c00 // 128
            S.op("vector", lambda e, ti=ti, b0=b0, nbt=nbt: e.tensor_copy(out=wTfull[:, b0:b0 + nbt, :], in_=pTb[ti][:, 0:nbt, :]),
                 reads=[("pTb", ti)], writes=[("wTfull", ci) for ci in cis])
        tr(0)
        for pj in range(len(pairs)):
            if pj + 1 < len(pairs):
                tr(pj + 1)
            for ci in pairs[pj]:
                u["pv"](ci, chunks[ci][0], chunks[ci][1] // 128)

    def attn_s3_fin(u):
        u["fin"](u["oi"])
        if u.get("post"):
            u["post"]()

    def pr_s1a(u):
        par = u["par"]
        chunks = u["chunks"]
        u["pis"] = []
        for ci, (c0, w) in enumerate(chunks):
            pi = nxt("pA", 4)
            u["pis"].append(pi)
            last = (ci == len(chunks) - 1)
            u["qk"](pi, ci, c0, w, last)
            si = nxt("sp", 3)
            S.op("scalar", lambda e, pi=pi, si=si, w=w: e.activation(
                out=spb[si][:, 0:w], in_=pA[pi][:, 0:w], func=AF.Exp, scale=-1.0),
                reads=[("pA", pi)], writes=[("sp", si)], extra=u.get("extra", ()))
            S.op("scalar", lambda e, si=si, par=par, c0=c0, w=w: e.activation(
                out=nlbS[par][:, 1 + c0:1 + c0 + w], in_=spb[si][:, 0:w], func=AF.Ln, bias=1.0, scale=1.0),
                reads=[("sp", si)], writes=[("nlb", par, ci)])

    def pr_s1b(u):
        par = u["par"]
        nk = u["nk"]
        chunks = u["chunks"]
        for ci, (c0, w) in enumerate(chunks):
            pi = u["pis"][ci]
            rk = [("pA", pi), ("nlb", par, ci)] + ([("nlb", par, ci - 1), ("u", par, ci - 1)] if ci else [("nlb0", par)])
            if ci == 0:
                S.op("vector", lambda e, pi=pi, par=par, w=w: e.tensor_tensor_scan(
                    out=ubuf[par][:, 0:w], data0=pA[pi][:, 0:w], data1=nlbS[par][:, 0:w], initial=0.0,
                    op0=ALU.add, op1=ALU.add), reads=rk, writes=[("u", par, ci)])
            else:
                S.op("vector", lambda e, pi=pi, par=par, c0=c0, w=w: e.tensor_tensor_scan(
                    out=ubuf[par][:, c0:c0 + w], data0=pA[pi][:, 0:w], data1=nlbS[par][:, c0:c0 + w],
                    initial=ubuf[par][:, c0 - 1:c0], op0=ALU.add, op1=ALU.add), reads=rk, writes=[("u", par, ci)])
        lastc = len(chunks) - 1
        allk = [("u", par, ci) for ci in range(len(chunks))] + [("nlb", par, ci) for ci in range(len(chunks))]
        S.op("vector", lambda e, par=par, nk=nk: e.tensor_tensor(
            out=twin[:], in0=ubuf[par][:, nk - 129:nk - 1], in1=nlbS[par][:, nk - 128:nk], op=ALU.add),
            reads=allk, writes=["twin"])
        S.op("vector", lambda e, par=par: e.scalar_tensor_tensor(
            out=tjunk[:], in0=twin[:], scalar=1.0, in1=nident_f[:], op0=ALU.mult, op1=ALU.mult,
            accum_out=negT[:, par:par + 1]),
            reads=["twin", "nident_f"], writes=["tjunk", ("negT", par)])

    def pr_s2(u):
        par = u["par"]
        for ci, (c0, w) in enumerate(u["chunks"]):
            S.op("scalar", lambda e, par=par, c0=c0, w=w: e.activation(
                out=wfull[par][:, c0:c0 + w], in_=ubuf[par][:, c0:c0 + w], func=AF.Exp,
                bias=negT[:, par:par + 1], scale=1.0),
                reads=[("u", par, ci), ("negT", par)], writes=[("wf", par, ci)])

    def run_pipeline(units):
        n = len(units)
        for k_ in range(n + 2):
            if k_ < n:
                units[k_]["s1a"](units[k_])
            if k_ >= 2:
                attn_s3(units[k_ - 2])
            if k_ < n:
                units[k_]["s1b"](units[k_])
            if 1 <= k_ <= n:
                units[k_ - 1]["s2"](units[k_ - 1])
            if k_ >= 2:
                attn_s3_fin(units[k_ - 2])

    def load_cache(s, extra=()):
        bi = s % 2
        S.dma("gpsimd", f"ckb{bi}", ckb[bi][:], ck[s].rearrange("(t p) f -> p t f", p=128),
              writes=[("ckb", bi)], extra=extra)
        S.dma("gpsimd", f"cvb{bi}", cvb[bi][:], cv[s].rearrange("(t p) f -> p t f", p=128),
              writes=[("cvb", bi)], extra=extra)

    def make_prompt_unit(h, i, par):
        nk = 1024 + 128 * (i + 1)
        chunks = [(c0, min(512, nk - c0)) for c0 in range(0, nk, 512)]
        tcols = slice(i * 128, (i + 1) * 128)
        u = {"par": par, "nk": nk, "chunks": chunks, "s1a": pr_s1a, "s1b": pr_s1b, "s2": pr_s2}

        def qk(pi, ci, c0, w, last):
            rk = [("qT", h), ("kT", h)]
            if not last:
                S.op("tensor", lambda e: e.matmul(pA[pi][:, 0:w], lhsT=qT[:, h, tcols], rhs=kT[:, h, c0:c0 + w],
                                                  start=True, stop=True),
                     reads=rk, writes=[("pA", pi)], inc=True)
                return
            wd = w - 128
            if wd > 0:
                S.op("tensor", lambda e: e.matmul(pA[pi][:, 0:wd], lhsT=qT[:, h, tcols], rhs=kT[:, h, c0:c0 + wd],
                                                  start=True, stop=True),
                     reads=rk, writes=[("pA", pi)], inc=False)
            S.op("tensor", lambda e: e.matmul(pA[pi][:, wd:w], lhsT=qT[:, h, tcols], rhs=kT[:, h, c0 + wd:c0 + w],
                                              start=True, stop=False),
                 reads=rk, writes=[("pA", pi)], inc=False)
            S.op("tensor", lambda e: e.matmul(pA[pi][:, wd:w], lhsT=maskTd[:], rhs=ident_bf[:],
                                              start=False, stop=True),
                 reads=["maskTd", "ident_bf"], writes=[("pA", pi)], inc=True)

        state = {"blk": 0}

        def pv(ci, c0, nb):
            oi = u["oi"]
            for b in range(nb):
                blk = c0 // 128 + b
                lastb = (blk == nk // 128 - 1)
                S.op("tensor", lambda e, blk=blk, lastb=lastb: e.matmul(
                    pTf[oi][:, 0, :], lhsT=v_hm[:, h, blk, :], rhs=wTfull[:, blk, :],
                    start=(blk == 0), stop=lastb),
                    reads=[("wTfull", ci), ("v", blk)], writes=[("pTf", oi)], inc=(lastb or b == nb - 1))

        def fin(oi):
            S.op("vector", lambda e: e.tensor_tensor(out=mixsb[:, h, tcols], in0=pTf[oi][:, 0, :],
                                                     in1=gsb[:, h, tcols], op=ALU.mult),
                 reads=[("pTf", oi), ("gsb", h)], writes=[("mixsb", h)])
        u["qk"] = qk; u["pv"] = pv; u["fin"] = fin
        return u

    for i in range(2):
        S.op("gpsimd", lambda e, i=i: e.memset(nlbS[i][:, 0:1], 0.0), writes=[("nlb0", i)], extra=t_proj_done)
    units = []
    n = 0
    for h in range(NH):
        for i in range(8):
            units.append(make_prompt_unit(h, i, n % 2))
            n += 1
    units[0]["extra"] = t_proj_done
    units[1]["extra"] = t_proj_done
    units[4 * 8 - 1]["post"] = (lambda: load_cache(0, extra=S.snapshot()))
    run_pipeline(units)
    t_attn_done = S.snapshot()

    def issue_wo(cc, extra=()):
        S.dma("gpsimd", f"wo{cc % 2}", wo[cc % 2][:].rearrange("p c f -> p (c f)"), w_out[cc],
              writes=[("wo", cc % 2)], extra=extra)

    def make_sample_unit(s, g, par):
        nk = 1152
        chunks = [(0, 512), (512, 512), (1024, 128)]
        bi = s % 2
        qcols = slice(SMP0 + 32 * s, SMP0 + 32 * s + 32)
        scols = slice(32 * s, 32 * s + 32)
        u = {"par": par, "nk": nk, "chunks": chunks, "e": esb, "P": psb, "ek": "es", "pk": "Ps",
             "s1a": attn_s1a, "s1b": attn_s1b, "s2": attn_s2}

        def qk(pi, ci, c0, w, last):
            for j in range(4):
                h = 4 * g + j
                rhs = kTp[:, h, c0:c0 + w] if ci < 2 else kT_s[:, h, :]
                rk = ["q_s", ("kTp", h)] if ci < 2 else ["q_s", ("kT_s", h)]
                S.op("tensor", lambda e, j=j, h=h, rhs=rhs: e.matmul(
                    pA[pi][32 * j:32 * j + 32, 0:w], lhsT=q_s[:, h, scols], rhs=rhs,
                    start=True, stop=(not last), tile_position=(0, 32 * j), skip_group_check=True),
                    reads=rk, writes=[("pA", pi)], inc=(j == 3 and not last))
            if last:
                S.op("tensor", lambda e: e.matmul(pA[pi][:, 0:128], lhsT=maskTs[s][:], rhs=ident_bf[:],
                                                  start=False, stop=True, skip_group_check=True),
                     reads=[("maskTs", s), "ident_bf"], writes=[("pA", pi)], inc=True)

        def pv(ci, c0, nb):
            pass

        def fin(oi):
            for j in range(4):
                h = 4 * g + j
                for blk in range(9):
                    if blk < 8:
                        lhsT = cvb[bi][:, blk, h * 128:(h + 1) * 128]
                        rk = [("cvb", bi)]
                    else:
                        lhsT = v_s[:, h * 128:(h + 1) * 128]
                        rk = [("v_s", h)]
                    S.op("tensor", lambda e, j=j, blk=blk, lhsT=lhsT: e.matmul(
                        pTf[oi][:, 0, 32 * j:32 * j + 32], lhsT=lhsT, rhs=wTfull[:, blk, 32 * j:32 * j + 32],
                        start=(blk == 0), stop=(blk == 8), skip_group_check=True),
                        reads=rk + [("wTfull", 0), ("wTfull", 1), ("wTfull", 2)], writes=[("pTf", oi)],
                        inc=(blk == 8 and j == 3))
            S.op("vector", lambda e: e.tensor_tensor(
                out=mixsb[:, 4 * g:4 * g + 4, qcols],
                in0=pTf[oi][:, 0, :].rearrange("p (j q) -> p j q", j=4),
                in1=g_s[:, 4 * g:4 * g + 4, scols], op=ALU.mult),
                reads=[("pTf", oi), "g_s"],
                writes=[("mixsb", 4 * g + j) for j in range(4)])
        u["qk"] = qk; u["pv"] = pv; u["fin"] = fin
        return u

    def transpose_kcache(s, g):
        bi = s % 2
        ex = t_attn_done if s == 0 else ()
        for h in range(4 * g, 4 * g + 4):
            for half in range(2):
                ti = nxt("pTb", 2)
                for b in range(4):
                    t = half * 4 + b
                    S.op("tensor", lambda e, t=t, b=b, ti=ti, h=h: e.transpose(
                        out=pTb[ti][:, b, :], in_=ckb[bi][:, t, h * 128:(h + 1) * 128], identity=ident_bf[:]),
                        reads=[("ckb", bi), "ident_bf"], writes=[("pTb", ti)], inc=(b == 3))
                dst = kTp[:, h, half * 512:(half + 1) * 512].rearrange("p (b k) -> p b k", b=4)
                S.op("vector", lambda e, dst=dst, ti=ti: e.tensor_copy(out=dst, in_=pTb[ti][:, 0:4, :]),
                     reads=[("pTb", ti)], writes=[("kTp", h)], extra=ex)

    load_cache(1, extra=t_attn_done)
    S.op("gpsimd", lambda e: e.tensor_copy(out=q_s[:], in_=qT[:, :, SMP0:SMP0 + 128]),
         reads=[("qT", h) for h in range(NH)], writes=["q_s"], extra=t_attn_done)
    S.op("gpsimd", lambda e: e.tensor_copy(out=g_s[:], in_=gsb[:, :, SMP0:SMP0 + 128]),
         reads=[("gsb", h) for h in range(NH)], writes=["g_s"], extra=t_attn_done)
    t_r1_free = S.snapshot()
    for i in range(2):
        S.op("gpsimd", lambda e, i=i: e.memset(psb[i][:, 0:1], 0.0), writes=[("Ps", i)], extra=t_attn_done)
    sunits = []
    for s in range(4):
        us = [make_sample_unit(s, g, g) for g in range(2)]
        us[0]["pre"] = (lambda s=s: transpose_kcache(s, 0))
        us[1]["pre"] = (lambda s=s: transpose_kcache(s, 1))
        if s == 0:
            us[1]["post"] = (lambda: load_cache(2))
        elif s == 1:
            us[1]["post"] = (lambda: (load_cache(3), issue_wo(0, extra=t_r1_free), issue_wo(1, extra=t_r1_free)))
        sunits += us
    sunits[0]["extra"] = t_attn_done
    sunits[1]["extra"] = t_attn_done
    run_pipeline(sunits)
    t_samp_done = S.snapshot()

    for tt in range(9):
        r0 = TC + tt * 128
        S.dma("sync", f"xres{tt}", yv(tt), xa[r0:r0 + 128, :], writes=[("yacc", tt)], extra=t_samp_done)
    S.dma("sync", "const2", fgb[:], final_g.partition_broadcast(128), writes=["fgb"], extra=t_samp_done)

    S.group_begin("sync", "yout")

    def tokcols(tt):
        return slice(tt * 128, (tt + 1) * 128) if tt < 8 else slice(SMP0, SMP0 + 128)

    for cc in range(4):
        for tt in range(9):
            pi = nxt("pA", 4)
            tc_ = tokcols(tt)
            for ec in range(16):
                lhsT = mixsb[:, ec, tc_] if ec < 8 else mixcv[:, ec - 8, tc_]
                rk = [("mixsb", ec)] if ec < 8 else [("mixcv", ec - 8)]
                S.op("tensor", lambda e, pi=pi, lhsT=lhsT, ec=ec, cc=cc: e.matmul(
                    pA[pi][:, 0:512], lhsT=lhsT, rhs=wo[cc % 2][:, ec, :], start=(ec == 0), stop=(ec == 15)),
                    reads=rk + [("wo", cc % 2)], writes=[("pA", pi)], inc=(ec == 15),
                    extra=(t_samp_done if (cc == 0 and tt == 0 and ec == 0) else ()))
            S.op("vector", lambda e, pi=pi, tt=tt, cc=cc: e.tensor_tensor(
                out=yv(tt, cc * 512, (cc + 1) * 512), in0=pA[pi][:, 0:512],
                in1=yv(tt, cc * 512, (cc + 1) * 512), op=ALU.add),
                reads=[("pA", pi), ("yacc", tt)], writes=[("yacc", tt)])
            if cc == 3:
                st = 17 + tt
                S.op("scalar", lambda e, tt=tt, st=st: e.activation(
                    out=junk2[:], in_=yv(tt), func=AF.Square, accum_out=stat[:, st:st + 1]),
                    reads=[("yacc", tt)], writes=["junk2", ("ss", st)])
                S.op("scalar", lambda e, st=st: e.activation(
                    out=stat[:, 32 + st:33 + st], in_=stat[:, st:st + 1], func=AF.Ln, scale=1.0 / D, bias=EPS),
                    reads=[("ss", st)], writes=[("lr", st)])
                S.op("scalar", lambda e, st=st: e.activation(
                    out=stat[:, 64 + st:65 + st], in_=stat[:, 32 + st:33 + st], func=AF.Exp, scale=-0.5),
                    reads=[("lr", st)], writes=[("rr", st)])
                S.op("vector", lambda e, tt=tt, st=st: e.scalar_tensor_tensor(
                    out=yv(tt), in0=yv(tt), scalar=stat[:, 64 + st:65 + st], in1=fgb[:],
                    op0=ALU.mult, op1=ALU.mult),
                    reads=[("yacc", tt), ("rr", st), "fgb"], writes=[("yacc", tt)])
                dst = yp[tt * 128:(tt + 1) * 128, :] if tt < 8 else ys[:, :]
                S.dma("sync", "yout", dst, yv(tt), reads=[("yacc", tt)])
        if cc + 2 < 4:
            issue_wo(cc + 2)

    S.group_end("yout")
    return S


_PROGRAM = None


def _get_program():
    global _PROGRAM
    if _PROGRAM is None:
        _PROGRAM = build_program()
    return _PROGRAM


def kernel(x_prompt, x_sample, cache_k, cache_v, state_conv, norm_g, w_in, conv_w, w_out, final_g):
    f = lambda a: np.ascontiguousarray(np.asarray(a, dtype=np.float32))
    x_prompt, x_sample = f(x_prompt), f(x_sample)
    cache_k, cache_v, state_conv = f(cache_k), f(cache_v), f(state_conv)
    norm_g, w_in, conv_w, w_out, final_g = f(norm_g), f(w_in), f(conv_w), f(w_out), f(final_g)
    nc = build_program()
    w3 = w_in[0].reshape(NDC, 128, 8192)
    w_in_r = np.empty((64, 128, NDC * 128), np.float32)
    for ci, f in enumerate(_forder_cols()):
        w_in_r[ci] = w3[:, :, f * 128:(f + 1) * 128].transpose(1, 0, 2).reshape(128, NDC * 128)
    wo3 = w_out[0].reshape(NDC, 128, D)
    w_out_r = np.empty((4, 128, NDC * 512), np.float32)
    for cc in range(4):
        w_out_r[cc] = wo3[:, :, cc * 512:(cc + 1) * 512].transpose(1, 0, 2).reshape(128, NDC * 512)
    in_maps = []
    for c in range(NCORES):
        b, half = c // 2, c % 2
        xa = np.zeros((TC + TO + TS, D), np.float32)
        if half == 1:
            xa[0:TC] = x_prompt[b, 0:1024]
        xa[TC:TC + TO] = x_prompt[b, half * 1024:(half + 1) * 1024]
        xa[TC + TO:] = x_sample[4 * c:4 * c + 4].reshape(TS, D)
        in_maps.append({
            "xa": xa,
            "ck": np.ascontiguousarray(cache_k[0, 4 * c:4 * c + 4].reshape(4, 1024, 1024)),
            "cv": np.ascontiguousarray(cache_v[0, 4 * c:4 * c + 4].reshape(4, 1024, 1024)),
            "sc": np.ascontiguousarray(state_conv[0, 4 * c:4 * c + 4]),
            "norm_g": norm_g[0], "final_g": final_g, "conv_w": conv_w[0],
            "w_in": w_in_r, "w_out": w_out_r,
        })
    res = run_bass_kernel_spmd(nc, in_maps, core_ids=list(range(NCORES)))
    R = res.results
    y_prompt = np.zeros((4, 2048, D), np.float32)
    y_sample = np.zeros((32, 32, D), np.float32)
    k_prompt = np.zeros((1, 4, 2048, 8, 128), np.float32)
    v_prompt = np.zeros((1, 4, 2048, 8, 128), np.float32)
    conv_prompt = np.zeros((1, 4, 2, 1024), np.float32)
    k_sample = np.zeros((1, 32, 32, 8, 128), np.float32)
    v_sample = np.zeros((1, 32, 32, 8, 128), np.float32)
    conv_sample = np.zeros((1, 32, 2, 1024), np.float32)
    for c in range(NCORES):
        b, half = c // 2, c % 2
        r = R[c]
        sl = slice(half * 1024, (half + 1) * 1024)
        y_prompt[b, sl] = r["yp"]
        k_prompt[0, b, sl] = r["kp"].reshape(1024, 8, 128)
        v_prompt[0, b, sl] = r["vp"].reshape(1024, 8, 128)
        if half == 1:
            conv_prompt[0, b] = r["cp"]
        y_sample[4 * c:4 * c + 4] = r["ys"].reshape(4, 32, D)
        k_sample[0, 4 * c:4 * c + 4] = r["ksn"].reshape(4, 32, 8, 128)
        v_sample[0, 4 * c:4 * c + 4] = r["vsn"].reshape(4, 32, 8, 128)
        conv_sample[0, 4 * c:4 * c + 4] = r["cs"]
    return (y_prompt, y_sample, k_prompt, v_prompt, conv_prompt, k_sample, v_sample, conv_sample)
```

```python
import numpy as np
import concourse.bass as bass
import concourse.mybir as mybir
from concourse.bass_utils import run_bass_kernel_spmd

F32 = mybir.dt.float32
BF16 = mybir.dt.bfloat16
AF = mybir.ActivationFunctionType
ALU = mybir.AluOpType

NCORES = 8
D = 2048
NDC = 16
TC = 1024
TO = 1024
TS = 128
HALO0 = TO
SMP0 = TO + 2
TA = TO + 2 + TS
NH = 8
SCALE = 128.0 ** -0.5
NEG = -30000.0
NEGD = -30.0
EPS = 1e-6
SB_BASE = 16640
SB_END = 229376
NWSLOT = 4


class Sched:
    ENG = ["sync", "scalar", "vector", "gpsimd", "tensor"]

    def __init__(self, nc):
        self.nc = nc
        self.ops = {e: [] for e in self.ENG}
        self.sem = {e: nc.alloc_semaphore("prog_" + e) for e in self.ENG}
        self.cnt = {e: 0 for e in self.ENG}
        self.seen = {e: {} for e in self.ENG}
        self.dsem = {}
        self.dcnt = {}
        self.lastw = {}
        self.readers = {}
        self.pending = {e: [] for e in self.ENG}
        self.groups = {}

    def _deps_for(self, reads, writes):
        deps = []
        for k in reads:
            w = self.lastw.get(k)
            if w is not None:
                deps.append(w)
        for k in writes:
            w = self.lastw.get(k)
            if w is not None:
                deps.append(w)
            deps.extend(self.readers.get(k, {}).values())
        return deps

    def _waits(self, eng, deps):
        waits = []
        for d in deps:
            if d is None:
                continue
            if d[1] is None:
                assert d[2] == eng == "tensor", (d, eng)
                continue
            sem, val = d[0], d[1]
            k = sem.name
            if self.seen[eng].get(k, 0) >= val:
                continue
            self.seen[eng][k] = val
            waits.append((sem, val))
        return waits

    def _register(self, tok, reads, writes):
        for k in reads:
            r = self.readers.setdefault(k, {})
            name = tok[0].name if tok[0] is not None else ("pend_" + tok[2])
            r[name] = tok
        for k in writes:
            self.lastw[k] = tok
            self.readers[k] = {}

    def op(self, eng, fn, reads=(), writes=(), inc=True, extra=()):
        deps = self._deps_for(reads, writes) + [x for x in extra if x is not None]
        waits = self._waits(eng, deps)
        if inc:
            self.cnt[eng] += 1
            tok = [self.sem[eng], self.cnt[eng], eng]
            for p in self.pending[eng]:
                p[0], p[1] = tok[0], tok[1]
            self.pending[eng] = []
        else:
            assert eng == "tensor"
            tok = [None, None, eng]
            self.pending[eng].append(tok)
        self._register(tok, reads, writes)
        self.ops[eng].append((fn, waits, tok if inc else None, 1))
        return tok

    def dma(self, eng, key, out, in_, reads=(), writes=(), extra=(), small=False):
        if key not in self.dsem:
            self.dsem[key] = self.nc.alloc_semaphore("dma_" + key)
            self.dcnt[key] = 0
        deps = self._deps_for(reads, writes) + [x for x in extra if x is not None]
        if key not in self.groups and self.dcnt[key] > 0:
            deps.append([self.dsem[key], self.dcnt[key], "dma"])
        waits = self._waits(eng, deps)
        self.dcnt[key] += 16
        tok = [self.dsem[key], self.dcnt[key], "dma"]
        if key in self.groups:
            self.groups[key].append(tok)
        self._register(tok, reads, writes)
        nc = self.nc

        def fn(e):
            if small:
                with nc.allow_non_contiguous_dma(reason="tiny strided transfer"):
                    return e.dma_start(out=out, in_=in_)
            return e.dma_start(out=out, in_=in_)
        self.ops[eng].append((fn, waits, tok, 16))
        return tok

    def group_begin(self, eng, key):
        if key not in self.dsem:
            self.dsem[key] = self.nc.alloc_semaphore("dma_" + key)
            self.dcnt[key] = 0
        assert key not in self.groups
        self.groups[key] = []
        if self.dcnt[key] > 0:
            w = self._waits(eng, [[self.dsem[key], self.dcnt[key], "dma"]])
            if w:
                self.ops[eng].append((lambda e: e.nop(), w, None, 0))

    def group_end(self, key):
        for t in self.groups.pop(key):
            t[1] = self.dcnt[key]

    def snapshot(self):
        toks = []
        assert not self.groups, "open DMA group at snapshot"
        for e in self.ENG:
            assert not self.pending[e], "pending PE tokens at snapshot"
            if self.cnt[e] > 0:
                toks.append([self.sem[e], self.cnt[e], e])
        for k, s in self.dsem.items():
            toks.append([s, self.dcnt[k], "dma"])
        return toks

    def all_dma_tokens(self):
        return [[s, self.dcnt[k], "dma"] for k, s in self.dsem.items()]

    def emit(self, block):
        final = self.all_dma_tokens()

        def mk(ename):
            def body(e):
                for fn, waits, tok, n in self.ops[ename]:
                    for sem, val in waits:
                        e.wait_ge(sem, val)
                    ins = fn(e)
                    if tok is not None:
                        ins.then_inc(tok[0], n)
                if ename == "sync":
                    for sem, val, _ in final:
                        e.wait_ge(sem, val)
            return body
        block.sync(mk("sync"))
        block.scalar(mk("scalar"))
        block.vector(mk("vector"))
        block.gpsimd(mk("gpsimd"))
        block.tensor(mk("tensor"))


class Arena:
    def __init__(self, nc):
        self.nc = nc
        self.n = 0

    def at(self, name, shape, dt, off):
        sz = int(np.prod(shape[1:])) * (4 if dt == F32 else 2)
        assert off % 32 == 0, (name, off)
        assert SB_BASE <= off and off + sz <= SB_END, (name, off, sz)
        self.n += 1
        return self.nc.alloc_sbuf_tensor_at(f"{name}_{self.n}", list(shape), dt, offset=off)


def _forder():
    fo = []
    for j in range(8):
        fo += [("c", j, 40 + j), ("u", j, 48 + j), ("b", j, 32 + j), ("gcv", j, 56 + j)]
        fo += [("k", j, 8 + j)]
    fo += [("v", h, 16 + h) for h in range(8)]
    fo += [("q", h, h) for h in range(8)]
    fo += [("gsb", h, 24 + h) for h in range(8)]
    return fo


def _forder_cols():
    return [f for (_, _, f) in _forder()]


def build_program():
    nc = bass.Bass("TRN2", target_bir_lowering=False)
    S = _build_body(nc)
    with nc.Block() as block:
        S.emit(block)
    return nc


def _build_body(nc):
    S = Sched(nc)
    A = Arena(nc)

    def din(name, shape):
        return nc.dram_tensor(name, list(shape), F32, kind="ExternalInput").ap()

    def dout(name, shape):
        return nc.dram_tensor(name, list(shape), F32, kind="ExternalOutput").ap()

    xa = din("xa", [TC + TO + TS, D])
    ck = din("ck", [4, 1024, 1024])
    cv = din("cv", [4, 1024, 1024])
    sc = din("sc", [4, 2, 1024])
    norm_g = din("norm_g", [D])
    final_g = din("final_g", [D])
    conv_w = din("conv_w", [3, 1024])
    w_in = din("w_in", [64, 128, NDC * 128])
    w_out = din("w_out", [4, 128, NDC * 512])
    yp = dout("yp", [TO, D])
    ys = dout("ys", [TS, D])
    kp = dout("kp", [TO, 1024])
    vp = dout("vp", [TO, 1024])
    cp = dout("cp", [2, 1024])
    ksn = dout("ksn", [TS, 1024])
    vsn = dout("vsn", [TS, 1024])
    cs = dout("cs", [4, 2, 1024])

    o = SB_BASE
    O_CONST = o; o += 6144
    O_R1 = o; o += 36928
    O_VT = o; o += 32768
    O_KT = o; o += 32800
    O_KTS = o; o += 4096
    O_MCV = o; o += 18464
    O_R2A = o; o += 18464
    O_R2B = o; o += 18464
    O_WR = o; o += NWSLOT * 4096
    O_WST = o; o += 2 * 8192
    O_TR = o
    assert O_TR + 10240 + 512 <= SB_END, O_TR
    O_ATT = O_R2B
    ATT_SZ = SB_END - O_ATT

    c = O_CONST
    ident_bf = A.at("ident_bf", [128, 128], BF16, c); c += 256
    ident_f = A.at("ident_f", [128, 128], F32, c); c += 512
    maskTd = A.at("maskTd", [128, 128], BF16, c); c += 256
    maskTs = []
    for s in range(4):
        maskTs.append(A.at(f"maskTs{s}", [128, 128], BF16, c)); c += 256
    ones = A.at("ones", [128, 512], F32, c); c += 2048
    convw = A.at("convw", [128, 8, 3], F32, c); c += 128
    scT = A.at("scT", [128, 8, 4, 2], F32, c); c += 256
    stat = A.at("stat", [128, 96], F32, c); c += 384
    negT = A.at("negT", [128, 2], F32, c); c += 32
    mtmp = A.at("mtmp", [128, 128], F32, c); c += 512
    nident_f = A.at("nident_f", [128, 128], F32, c); c += 512
    assert c <= O_CONST + 6144

    hTC = A.at("hTC", [128, NDC, TC], BF16, O_R1)
    qT = A.at("qT", [128, NH, TA], BF16, O_R1)
    gsb = A.at("gsb", [128, NH, TA], BF16, O_R1 + 18464)
    hTA = A.at("hTA", [128, NDC, TA], BF16, O_R2A)
    mixsb = A.at("mixsb", [128, NH, TA], BF16, O_R2A)
    v_hm = A.at("v_hm", [128, NH, 16, 128], BF16, O_VT)
    kT = A.at("kT", [128, NH, 2050], BF16, O_KT)
    kT_s = A.at("kT_s", [128, NH, 128], BF16, O_KTS)
    v_s = A.at("v_s", [128, 1024], BF16, O_KTS + 2048)
    mixcv = A.at("mixcv", [128, NH, TA], BF16, O_MCV)
    wring = [A.at(f"wr{i}", [128, NDC, 128], BF16, O_WR + 4096 * i) for i in range(NWSLOT)]
    wst = [A.at(f"wst{i}", [128, NDC, 128], F32, O_WST + 8192 * i) for i in range(2)]
    kfm = [A.at(f"kfm{i}", [128, 512], F32, O_TR + 2048 * i) for i in range(2)]
    ktm = [A.at(f"ktm{i}", [128, 4, 128], F32, O_TR + 4096 + 2048 * i) for i in range(3)]
    xt = [A.at(f"xt{i}", [128, D], F32, O_KT + 8192 * i) for i in range(2)]
    xt.append(A.at("xt2", [128, D], F32, O_MCV))
    xs = [A.at(f"xs{i}", [128, D], BF16, O_KT + 16384 + 4096 * i) for i in range(2)]
    junk = A.at("junk", [128, D], BF16, O_KT + 24576)
    gbc = A.at("gbc", [128, D], F32, O_KT + 28672)
    assert 28672 + 8192 <= 32800 + 4096
    c_sb = A.at("c_sb", [128, TA], F32, O_VT)
    cu_p = A.at("cu_p", [128, 1026], F32, O_VT + 4640)
    cu_s = A.at("cu_s", [128, 4, 34], F32, O_VT + 4640 + 4128)
    acc = A.at("acc", [128, 1152], F32, O_VT + 4640 + 4128 + 576)
    sg = A.at("sg", [128, TA], F32, O_VT + 4640 + 4128 + 576 + 4608)
    b_sb = A.at("b_sb", [128, TA], F32, O_VT + 4640 + 4128 + 576 + 4608 + 4640)
    a = O_ATT
    spb = []; gb_ = []; wfull = []
    for i in range(3):
        spb.append(A.at(f"sp{i}", [128, 512], F32, a)); a += 2048
    for i in range(4):
        gb_.append(A.at(f"g{i}", [128, 512], BF16, a)); a += 1024
    for i in range(2):
        wfull.append(A.at(f"wf{i}", [128, 2048], BF16, a)); a += 4096
    wTfull = A.at("wTfull", [128, 17, 128], BF16, a)
    g_s = A.at("g_s", [128, NH, 128], BF16, a + 2304)
    a += 4352
    a_ph = a
    nlbS = []; ubuf = []
    for i in range(2):
        ubuf.append(A.at(f"u{i}", [128, 2048], F32, a)); a += 8192
        nlbS.append(A.at(f"nlbS{i}", [128, 2056], F32, a)); a += 8224
    twin = A.at("twin", [128, 128], F32, a); a += 512
    tjunk = A.at("tjunk", [128, 128], F32, a); a += 512
    assert a <= SB_END, a
    a = a_ph
    esb = []; psb = []
    for i in range(2):
        esb.append(A.at(f"es{i}", [128, 1152], F32, a)); a += 4608
        psb.append(A.at(f"Ps{i}", [128, 1160], F32, a)); a += 4640
    kTp = A.at("kTp", [128, NH, 1024], BF16, a); a += 16384
    q_s = A.at("q_s", [128, NH, 128], BF16, a); a += 2048
    assert a <= SB_END, a
    ckb = [A.at(f"ckb{i}", [128, 8, 1024], BF16, O_VT + 16384 * i) for i in range(2)]
    cvb = [A.at(f"cvb{i}", [128, 8, 1024], BF16, O_VT + 32768 + 16384 * i) for i in range(2)]
    assert O_VT + 65536 <= O_KTS
    wo = [A.at(f"wo{i}", [128, NDC, 512], BF16, O_R1 + 16384 * i) for i in range(2)]
    yacc8 = A.at("yacc8", [128, 8, D], F32, O_VT)
    assert O_VT + 65536 <= O_MCV
    fgb = A.at("fgb", [128, D], F32, O_ATT)
    yacc_s = A.at("yacc_s", [128, D], F32, O_ATT + 12288)

    def yv(tt, lo=0, hi=D):
        return yacc8[:, tt, lo:hi] if tt < 8 else yacc_s[:, lo:hi]
    junk2 = A.at("junk2", [128, D], BF16, O_ATT + 8192)

    pA = [nc.alloc_psum_tensor(f"pA{i}", [128, 512], F32) for i in range(4)]
    pTf = [nc.alloc_psum_tensor(f"pTf{i}", [128, 4, 128], F32) for i in range(2)]
    pTb = [nc.alloc_psum_tensor(f"pTb{i}", [128, 8, 128], BF16) for i in range(2)]

    ring = {"pA": 0, "pTf": 0, "pTb": 0, "w": 0, "kfm": 0, "ktm": 0, "sp": 0, "g": 0, "x": 0, "xs": 0}

    def nxt(name, n):
        i = ring[name] % n
        ring[name] += 1
        return i

    S.op("gpsimd", lambda e: e.memset(mtmp[:], 0.0), writes=["mtmp"])
    S.op("gpsimd", lambda e: e.affine_select(out=ident_f[:], in_=mtmp[:], pattern=[[-1, 128]],
                                             compare_op=ALU.not_equal, fill=1.0, base=0,
                                             channel_multiplier=1), reads=["mtmp"], writes=["ident_f"])
    S.op("gpsimd", lambda e: e.tensor_copy(out=ident_bf[:], in_=ident_f[:]), reads=["ident_f"], writes=["ident_bf"])
    S.op("gpsimd", lambda e: e.affine_select(out=maskTd[:], in_=mtmp[:], pattern=[[1, 128]],
                                             compare_op=ALU.is_gt, fill=NEGD, base=0,
                                             channel_multiplier=-1), reads=["mtmp"], writes=["maskTd"])
    S.op("gpsimd", lambda e: e.affine_select(out=nident_f[:], in_=mtmp[:], pattern=[[-1, 128]],
                                             compare_op=ALU.not_equal, fill=-1.0, base=0,
                                             channel_multiplier=1), reads=["mtmp"], writes=["nident_f"])
    S.op("gpsimd", lambda e: e.memset(ones[:], 1.0), writes=["ones"])

    msk_tmp = A.at("msk_tmp", [128, 128], F32, O_TR + 10240)
    for s in range(4):
        S.op("gpsimd", lambda e, s=s: e.affine_select(out=msk_tmp[:], in_=mtmp[:], pattern=[[0, 4], [0, 32]],
                                                      compare_op=ALU.is_ge, fill=NEG, base=-32 * s,
                                                      channel_multiplier=1),
             reads=["mtmp", ("maskTs", s - 1)], writes=["msk_tmp"])
        S.op("gpsimd", lambda e, s=s: e.affine_select(out=msk_tmp[:], in_=msk_tmp[:], pattern=[[0, 4], [1, 32]],
                                                      compare_op=ALU.is_gt, fill=NEG, base=32 * s,
                                                      channel_multiplier=-1),
             reads=["msk_tmp"], writes=["msk_tmp"])
        S.op("gpsimd", lambda e, s=s: e.tensor_copy(out=maskTs[s][:], in_=msk_tmp[:]),
             reads=["msk_tmp"], writes=[("maskTs", s)])
    S.dma("sync", "gbc", gbc[:], norm_g.partition_broadcast(128), writes=["gbc"])
    S.group_begin("gpsimd", "const")
    for j in range(8):
        S.dma("gpsimd", "const", convw[:, j, :], conv_w[:, j * 128:(j + 1) * 128].rearrange("i p -> p i"),
              writes=[("convw", j)], small=True)
        for s in range(4):
            S.dma("gpsimd", "const", scT[:, j, s, :], sc[s, :, j * 128:(j + 1) * 128].rearrange("r p -> p r"),
                  writes=[("scT", j, s)], small=True)
    S.group_end("const")


    forder = _forder()
    wload_state = {"next": 0, "pending_cast": []}

    def issue_wload(extra=()):
        i = wload_state["next"]
        if i >= len(forder):
            return
        wload_state["next"] += 1
        slot = i % NWSLOT
        sti = i % 2
        S.dma("sync", f"ws{sti}", wst[sti][:].rearrange("p c f -> p (c f)"), w_in[i], writes=[("wst", sti)], extra=extra)
        wload_state["pending_cast"].append((slot, sti))

    def issue_wcast():
        if not wload_state["pending_cast"]:
            return
        slot, sti = wload_state["pending_cast"].pop(0)
        S.op("scalar", lambda e, slot=slot, sti=sti: e.activation(
            out=wring[slot][:].rearrange("p c f -> p (c f)"), in_=wst[sti][:].rearrange("p c f -> p (c f)"), func=AF.Copy),
            reads=[("wst", sti)], writes=[("w", slot)])

    issue_wload(); issue_wload()
    issue_wcast(); issue_wcast()
    issue_wload(); issue_wload()

    tile_order = [7] + list(range(8, 17)) + list(range(0, 7))
    for n_i, tt in enumerate(tile_order):
        xi = nxt("x", 3)
        S.dma("sync", f"x{xi}", xt[xi][:], xa[tt * 128:(tt + 1) * 128, :], writes=[("xt", xi)])
        S.op("scalar", lambda e, xi=xi, tt=tt: e.activation(out=junk[:], in_=xt[xi][:], func=AF.Square,
                                                            accum_out=stat[:, tt:tt + 1]),
             reads=[("xt", xi)], writes=["junk", ("ss", tt)])
        S.op("scalar", lambda e, tt=tt: e.activation(out=stat[:, 32 + tt:33 + tt], in_=stat[:, tt:tt + 1],
                                                     func=AF.Ln, scale=1.0 / D, bias=EPS),
             reads=[("ss", tt)], writes=[("lr", tt)])
        S.op("scalar", lambda e, tt=tt: e.activation(out=stat[:, 64 + tt:65 + tt], in_=stat[:, 32 + tt:33 + tt],
                                                     func=AF.Exp, scale=-0.5),
             reads=[("lr", tt)], writes=[("rr", tt)])
        si = nxt("xs", 2)
        S.op("vector", lambda e, xi=xi, si=si, tt=tt: e.scalar_tensor_tensor(
            out=xs[si][:], in0=xt[xi][:], scalar=stat[:, 64 + tt:65 + tt], in1=gbc[:],
            op0=ALU.mult, op1=ALU.mult),
            reads=[("xt", xi), ("rr", tt), "gbc"], writes=[("xs", si)])
        for grp in range(4):
            pi = nxt("pTb", 2)
            for jj in range(4):
                dc = grp * 4 + jj
                S.op("tensor", lambda e, si=si, dc=dc, pi=pi, jj=jj: e.transpose(
                    out=pTb[pi][:, jj, :], in_=xs[si][:, dc * 128:(dc + 1) * 128], identity=ident_bf[:]),
                    reads=[("xs", si), "ident_bf"], writes=[("pTb", pi)], inc=(jj == 3))
            if tt < 8:
                dst = hTC[:, grp * 4:(grp + 1) * 4, tt * 128:(tt + 1) * 128]
                wkey = ("hTC", tt)
            elif tt < 16:
                dst = hTA[:, grp * 4:(grp + 1) * 4, (tt - 8) * 128:(tt - 7) * 128]
                wkey = ("hTA", tt - 8)
            else:
                dst = hTA[:, grp * 4:(grp + 1) * 4, SMP0:SMP0 + 128]
                wkey = ("hTA", 8)
            ceng = "vector"
            if ceng == "vector":
                S.op("vector", lambda e, dst=dst, pi=pi: e.tensor_copy(out=dst, in_=pTb[pi][:, 0:4, :]),
                     reads=[("pTb", pi)], writes=[wkey])
            else:
                S.op("scalar", lambda e, dst=dst, pi=pi: e.activation(out=dst, in_=pTb[pi][:, 0:4, :], func=AF.Copy),
                     reads=[("pTb", pi)], writes=[wkey])
            if tt == 7:
                hdst = hTA[:, grp * 4:(grp + 1) * 4, HALO0:HALO0 + 2]
                S.op("vector", lambda e, hdst=hdst, pi=pi: e.tensor_copy(out=hdst, in_=pTb[pi][:, 0:4, 126:128]),
                     reads=[("pTb", pi)], writes=[("hTA", 8)])
    t_norm_done = S.snapshot()

    def hkeys(buf, c0, w):
        if buf == "A":
            lo = min(c0 // 128, 8); hi = min((c0 + w - 1) // 128, 8)
            return [("hTA", t) for t in range(lo, hi + 1)]
        lo = c0 // 128; hi = (c0 + w - 1) // 128
        return [("hTC", t) for t in range(lo, hi + 1)]

    A_CHUNKS_KV = [(0, 512), (512, 512), (1024, TA - 1024)]
    A_CHUNKS_BAL = [(0, 385), (385, 385), (770, TA - 770)]
    C_CHUNKS = [(0, 512), (512, 512)]

    defer_q = []

    def proj_tile(slot, buf, c0, w):
        pi = nxt("pA", 4)
        src = hTA if buf == "A" else hTC
        rk = hkeys(buf, c0, w) + [("w", slot)]
        for dc in range(NDC):
            S.op("tensor", lambda e, pi=pi, slot=slot, dc=dc, src=src, c0=c0, w=w: e.matmul(
                pA[pi][:, 0:w], lhsT=wring[slot][:, dc, :], rhs=src[:, dc, c0:c0 + w],
                start=(dc == 0), stop=(dc == NDC - 1)),
                reads=rk, writes=[("pA", pi)], inc=(dc == NDC - 1))
        while defer_q:
            defer_q.pop(0)()
        return pi

    for ci, (kind, idx, f) in enumerate(forder):
        slot = ci % NWSLOT
        issue_wcast()
        issue_wload()
        extra_first = ()
        if kind == "k" and idx == 0:
            extra_first = t_norm_done
        if kind == "v" and idx == 0:
            extra_first = S.snapshot()
        if kind == "q" and idx == 0:
            extra_first = S.snapshot()
        if kind in ("k", "v"):
            for (c0, w) in C_CHUNKS:
                pi = proj_tile(slot, "C", c0, w)
                if kind == "k":
                    S.op("scalar", lambda e, pi=pi, idx=idx, c0=c0, w=w: e.activation(
                        out=kT[:, idx, c0:c0 + w], in_=pA[pi][:, 0:w], func=AF.Copy),
                        reads=[("pA", pi)], writes=[("kT", idx)], extra=extra_first)
                    extra_first = ()
                else:
                    ki = nxt("kfm", 2)
                    S.op("vector", lambda e, pi=pi, ki=ki, w=w: e.tensor_copy(out=kfm[ki][:, 0:w], in_=pA[pi][:, 0:w]),
                         reads=[("pA", pi)], writes=[("kfm", ki)])
                    def post_ctx(ki=ki, c0=c0, idx=idx, ex=extra_first):
                        ti = nxt("pTf", 2)
                        for b in range(4):
                            S.op("tensor", lambda e, ki=ki, ti=ti, b=b: e.transpose(
                                out=pTf[ti][:, b, :], in_=kfm[ki][:, b * 128:(b + 1) * 128], identity=ident_f[:]),
                                reads=[("kfm", ki), "ident_f"], writes=[("pTf", ti)], inc=(b == 3))
                        t0 = c0 // 128
                        S.op("vector", lambda e, ti=ti, t0=t0, idx=idx: e.tensor_copy(
                            out=v_hm[:, idx, t0:t0 + 4, :], in_=pTf[ti][:, 0:4, :]),
                            reads=[("pTf", ti)], writes=[("v", t) for t in range(t0, t0 + 4)], extra=ex)
                    defer_q.append(post_ctx)
                    extra_first = ()
        A_CHUNKS = A_CHUNKS_KV if kind in ("k", "v") else A_CHUNKS_BAL
        for cc, (c0, w) in enumerate(A_CHUNKS):
            pi = proj_tile(slot, "A", c0, w)
            own_len = max(0, min(c0 + w, TO) - c0)
            has_tail = (c0 + w == TA)
            if kind == "c":
                S.op("scalar", lambda e, pi=pi, c0=c0, w=w: e.activation(
                    out=c_sb[:, c0:c0 + w], in_=pA[pi][:, 0:w], func=AF.Copy),
                    reads=[("pA", pi)], writes=[("c_sb", cc)])
            elif kind == "u":
                S.op("vector", lambda e, pi=pi, c0=c0, n_=own_len: e.tensor_tensor(
                    out=cu_p[:, 2 + c0:2 + c0 + n_], in0=pA[pi][:, 0:n_], in1=c_sb[:, c0:c0 + n_], op=ALU.mult),
                    reads=[("pA", pi), ("c_sb", cc)], writes=[("cu", cc)])
                if has_tail:
                    ho = HALO0 - c0
                    so = SMP0 - c0
                    S.op("vector", lambda e, pi=pi, ho=ho: e.tensor_tensor(
                        out=cu_p[:, 0:2], in0=pA[pi][:, ho:ho + 2], in1=c_sb[:, HALO0:HALO0 + 2], op=ALU.mult),
                        reads=[("pA", pi), ("c_sb", 2)], writes=[("cu", 5)])
                    S.op("gpsimd", lambda e, idx=idx: e.tensor_copy(out=cu_s[:, :, 0:2], in_=scT[:, idx, :, :]),
                         reads=[("scT", idx, s_) for s_ in range(4)], writes=[("cu", 3)])
                    S.op("vector", lambda e, pi=pi, so=so: e.tensor_tensor(
                        out=cu_s[:, :, 2:34], in0=pA[pi][:, so:so + 128].rearrange("p (s t) -> p s t", s=4),
                        in1=c_sb[:, SMP0:SMP0 + 128].rearrange("p (s t) -> p s t", s=4), op=ALU.mult),
                        reads=[("pA", pi), ("c_sb", 2)], writes=[("cu", 4)])
                    S.group_begin("gpsimd", "cout")
                    S.dma("gpsimd", "cout", cp[:, idx * 128:(idx + 1) * 128].rearrange("r p -> p r"),
                          cu_p[:, 1024:1026], reads=[("cu", 2)], small=True)
                    for s in range(4):
                        S.dma("gpsimd", "cout", cs[s, :, idx * 128:(idx + 1) * 128].rearrange("r p -> p r"),
                              cu_s[:, s, 32:34], reads=[("cu", 4)], small=True)
                    S.group_end("cout")
                    allcu = [("cu", i) for i in range(6)]
                    S.op("vector", lambda e, idx=idx: e.tensor_scalar(
                        out=acc[:, 0:1024], in0=cu_p[:, 2:1026], scalar1=convw[:, idx, 2:3], scalar2=None,
                        op0=ALU.mult), reads=allcu + [("convw", idx)], writes=["acc"])
                    S.op("vector", lambda e, idx=idx: e.scalar_tensor_tensor(
                        out=acc[:, 0:1024], in0=cu_p[:, 1:1025], scalar=convw[:, idx, 1:2], in1=acc[:, 0:1024],
                        op0=ALU.mult, op1=ALU.add), reads=allcu + ["acc"], writes=["acc"])
                    S.op("vector", lambda e, idx=idx: e.scalar_tensor_tensor(
                        out=acc[:, 0:1024], in0=cu_p[:, 0:1024], scalar=convw[:, idx, 0:1], in1=acc[:, 0:1024],
                        op0=ALU.mult, op1=ALU.add), reads=allcu + ["acc"], writes=["acc"])
                    accs = acc[:, 1024:1152].rearrange("p (s t) -> p s t", s=4)
                    S.op("vector", lambda e, idx=idx, accs=accs: e.tensor_scalar(
                        out=accs, in0=cu_s[:, :, 2:34], scalar1=convw[:, idx, 2:3], scalar2=None,
                        op0=ALU.mult), reads=allcu + ["acc"], writes=["acc"])
                    S.op("vector", lambda e, idx=idx, accs=accs: e.scalar_tensor_tensor(
                        out=accs, in0=cu_s[:, :, 1:33], scalar=convw[:, idx, 1:2], in1=accs,
                        op0=ALU.mult, op1=ALU.add), reads=allcu + ["acc"], writes=["acc"])
                    S.op("vector", lambda e, idx=idx, accs=accs: e.scalar_tensor_tensor(
                        out=accs, in0=cu_s[:, :, 0:32], scalar=convw[:, idx, 0:1], in1=accs,
                        op0=ALU.mult, op1=ALU.add), reads=allcu + ["acc"], writes=["acc"])
            elif kind == "b":
                S.op("scalar", lambda e, pi=pi, c0=c0, w=w: e.activation(
                    out=b_sb[:, c0:c0 + w], in_=pA[pi][:, 0:w], func=AF.Copy),
                    reads=[("pA", pi)], writes=[("b_sb", cc)])
                if has_tail:
                    S.op("gpsimd", lambda e: e.tensor_tensor(
                        out=acc[:, 0:1024], in0=acc[:, 0:1024], in1=b_sb[:, 0:1024], op=ALU.mult),
                        reads=["acc", ("b_sb", 0), ("b_sb", 1), ("b_sb", 2)], writes=["acc"])
                    S.op("gpsimd", lambda e: e.tensor_tensor(
                        out=acc[:, 1024:1152], in0=acc[:, 1024:1152], in1=b_sb[:, SMP0:SMP0 + 128], op=ALU.mult),
                        reads=["acc", ("b_sb", 2)], writes=["acc"])
            elif kind == "gcv":
                S.op("scalar", lambda e, pi=pi, c0=c0, w=w: e.activation(
                    out=sg[:, c0:c0 + w], in_=pA[pi][:, 0:w], func=AF.Silu),
                    reads=[("pA", pi)], writes=[("sg", cc)])
                if has_tail:
                    S.op("gpsimd", lambda e, idx=idx: e.tensor_tensor(
                        out=mixcv[:, idx, 0:1024], in0=acc[:, 0:1024], in1=sg[:, 0:1024], op=ALU.mult),
                        reads=["acc", ("sg", 0), ("sg", 1), ("sg", 2)], writes=[("mixcv", idx)],
                        extra=(t_norm_done if idx == 0 else ()))
                    S.op("gpsimd", lambda e, idx=idx: e.tensor_tensor(
                        out=mixcv[:, idx, SMP0:SMP0 + 128], in0=acc[:, 1024:1152], in1=sg[:, SMP0:SMP0 + 128], op=ALU.mult),
                        reads=["acc", ("sg", 2)], writes=[("mixcv", idx)])
            elif kind == "q":
                S.op("scalar", lambda e, pi=pi, idx=idx, c0=c0, w=w: e.activation(
                    out=qT[:, idx, c0:c0 + w], in_=pA[pi][:, 0:w], func=AF.Copy, scale=SCALE),
                    reads=[("pA", pi)], writes=[("qT", idx)], extra=extra_first)
                extra_first = ()
            elif kind == "gsb":
                S.op("scalar", lambda e, pi=pi, idx=idx, c0=c0, w=w: e.activation(
                    out=gsb[:, idx, c0:c0 + w], in_=pA[pi][:, 0:w], func=AF.Silu),
                    reads=[("pA", pi)], writes=[("gsb", idx)])
            elif kind in ("k", "v"):
                nb = 4 if cc < 2 else 1
                pc0 = 0 if cc < 2 else 2
                pw = 512 if cc < 2 else 128
                if kind == "k":
                    if cc < 2:
                        S.op("scalar", lambda e, pi=pi, idx=idx, c0=c0: e.activation(
                            out=kT[:, idx, 1024 + c0:1024 + c0 + 512], in_=pA[pi][:, 0:512], func=AF.Copy),
                            reads=[("pA", pi)], writes=[("kT", idx)])
                    else:
                        S.op("scalar", lambda e, pi=pi, idx=idx: e.activation(
                            out=kT_s[:, idx, :], in_=pA[pi][:, 2:130], func=AF.Copy),
                            reads=[("pA", pi)], writes=[("kT_s", idx)])
                ki = nxt("kfm", 2)
                S.op("scalar", lambda e, pi=pi, ki=ki, pc0=pc0, pw=pw: e.activation(
                    out=kfm[ki][:, 0:pw], in_=pA[pi][:, pc0:pc0 + pw], func=AF.Copy),
                    reads=[("pA", pi)], writes=[("kfm", ki)])
                def post_own(ki=ki, nb=nb, kind=kind, cc=cc, c0=c0, idx=idx):
                    ti = nxt("pTf", 2)
                    for b in range(nb):
                        S.op("tensor", lambda e, ki=ki, ti=ti, b=b: e.transpose(
                            out=pTf[ti][:, b, :], in_=kfm[ki][:, b * 128:(b + 1) * 128], identity=ident_f[:]),
                            reads=[("kfm", ki), "ident_f"], writes=[("pTf", ti)], inc=(b == nb - 1))
                    if kind == "v":
                        if cc < 2:
                            t0 = 8 + c0 // 128
                            S.op("vector", lambda e, ti=ti, t0=t0, idx=idx: e.tensor_copy(
                                out=v_hm[:, idx, t0:t0 + 4, :], in_=pTf[ti][:, 0:4, :]),
                                reads=[("pTf", ti)], writes=[("v", t) for t in range(t0, t0 + 4)])
                        else:
                            S.op("vector", lambda e, ti=ti, idx=idx: e.tensor_copy(
                                out=v_s[:, idx * 128:(idx + 1) * 128], in_=pTf[ti][:, 0, :]),
                                reads=[("pTf", ti)], writes=[("v_s", idx)])
                    mi = nxt("ktm", 3)
                    S.op("vector", lambda e, ti=ti, mi=mi, nb=nb: e.tensor_copy(
                        out=ktm[mi][:, 0:nb, :], in_=pTf[ti][:, 0:nb, :]),
                        reads=[("pTf", ti)], writes=[("ktm", mi)])
                    if cc < 2:
                        dst = (kp if kind == "k" else vp)[c0:c0 + 512, idx * 128:(idx + 1) * 128].rearrange(
                            "(b p) f -> p b f", p=128)
                        S.dma("gpsimd", f"ktm{mi}", dst, ktm[mi][:], reads=[("ktm", mi)])
                    else:
                        dst = (ksn if kind == "k" else vsn)[:, idx * 128:(idx + 1) * 128]
                        S.dma("gpsimd", f"ktm{mi}", dst, ktm[mi][:, 0, :], reads=[("ktm", mi)])
                defer_q.append(post_own)

    while defer_q:
        defer_q.pop(0)()
    t_proj_done = S.snapshot()

    def attn_s1a(u):
        par = u["par"]
        ebuf, ek = u["e"], u["ek"]
        chunks = u["chunks"]
        if u.get("pre"):
            u["pre"]()
        for ci, (c0, w) in enumerate(chunks):
            pi = nxt("pA", 4)
            last = (ci == len(chunks) - 1)
            u["qk"](pi, ci, c0, w, last)
            S.op("scalar", lambda e, pi=pi, par=par, c0=c0, w=w: e.activation(
                out=ebuf[par][:, c0:c0 + w], in_=pA[pi][:, 0:w], func=AF.Exp, scale=1.0),
                reads=[("pA", pi)], writes=[(ek, par, ci)], extra=u.get("extra", ()))
        u["spi"] = []
        for ci, (c0, w) in enumerate(chunks):
            si = nxt("sp", 3)
            u["spi"].append(si)
            S.op("scalar", lambda e, si=si, par=par, c0=c0, w=w: e.activation(
                out=spb[si][:, 0:w], in_=ebuf[par][:, c0:c0 + w], func=AF.Ln, bias=1.0, scale=1.0),
                reads=[(ek, par, ci)], writes=[("sp", si)])

    def attn_s1b(u):
        par = u["par"]
        pbuf, pk = u["P"], u["pk"]
        for ci, (c0, w) in enumerate(u["chunks"]):
            si = u["spi"][ci]
            if ci == 0:
                S.op("vector", lambda e, si=si, par=par, w=w: e.tensor_tensor_scan(
                    out=pbuf[par][:, 1:1 + w], data0=ones[:, 0:w], data1=spb[si][:, 0:w], initial=0.0,
                    op0=ALU.mult, op1=ALU.add),
                    reads=[("sp", si), "ones"], writes=[(pk, par)])
            else:
                S.op("vector", lambda e, si=si, par=par, c0=c0, w=w: e.tensor_tensor_scan(
                    out=pbuf[par][:, 1 + c0:1 + c0 + w], data0=ones[:, 0:w], data1=spb[si][:, 0:w],
                    initial=pbuf[par][:, c0:c0 + 1], op0=ALU.mult, op1=ALU.add),
                    reads=[("sp", si), "ones", (pk, par)], writes=[(pk, par)])

    def attn_s2(u):
        par = u["par"]
        ebuf, pbuf, ek, pk = u["e"], u["P"], u["ek"], u["pk"]
        nk = u["nk"]
        S.op("scalar", lambda e, par=par, nk=nk: e.activation(
            out=negT[:, par:par + 1], in_=pbuf[par][:, nk:nk + 1], func=AF.Copy, scale=-1.0),
            reads=[(pk, par)], writes=[("negT", par)])
        for ci, (c0, w) in enumerate(u["chunks"]):
            gi = nxt("g", 4)
            S.op("scalar", lambda e, gi=gi, par=par, c0=c0, w=w: e.activation(
                out=gb_[gi][:, 0:w], in_=pbuf[par][:, c0:c0 + w], func=AF.Exp, bias=negT[:, par:par + 1], scale=1.0),
                reads=[(pk, par), ("negT", par)], writes=[("g", gi)])
            S.op("vector", lambda e, gi=gi, par=par, c0=c0, w=w: e.tensor_tensor(
                out=wfull[par][:, c0:c0 + w], in0=ebuf[par][:, c0:c0 + w], in1=gb_[gi][:, 0:w], op=ALU.mult),
                reads=[(ek, par, ci), ("g", gi)], writes=[("wf", par, ci)])

    def attn_s3(u):
        par = u["par"]
        oi = nxt("pTf", 2)
        u["oi"] = oi
        chunks = u["chunks"]

        pairs = [list(range(j, min(j + 2, len(chunks)))) for j in range(0, len(chunks), 2)]

        def tr(pj):
            cis = pairs[pj]
            ti = nxt("pTb", 2)
            c00 = chunks[cis[0]][0]
            nbt = sum(chunks[ci][1] // 128 for ci in cis)
            for bb in range(nbt):
                col = c00 + bb * 128
                S.op("tensor", lambda e, par=par, col=col, ti=ti, bb=bb: e.transpose(
                    out=pTb[ti][:, bb, :], in_=wfull[par][:, col:col + 128], identity=ident_bf[:]),
                    reads=[("wf", par, ci) for ci in cis] + ["ident_bf"], writes=[("pTb", ti)], inc=(bb == nbt - 1))
            b0 = c00 // 128
            S.op("vector", lambda e, ti=ti, b0=b0, nbt=nbt: e.tensor_copy(out=wTfull[:, b0:b0 + nbt, :], in_=pTb[ti][:, 0:nbt, :]),
                 reads=[("pTb", ti)], writes=[("wTfull", ci) for ci in cis])
        tr(0)
        for pj in range(len(pairs)):
            if pj + 1 < len(pairs):
                tr(pj + 1)
            for ci in pairs[pj]:
                u["pv"](ci, chunks[ci][0], chunks[ci][1] // 128)

    def attn_s3_fin(u):
        u["fin"](u["oi"])
        if u.get("post"):
            u["post"]()

    def pr_s1a(u):
        par = u["par"]
        chunks = u["chunks"]
        u["pis"] = []
        for ci, (c0, w) in enumerate(chunks):
            pi = nxt("pA", 4)
            u["pis"].append(pi)
            last = (ci == len(chunks) - 1)
            u["qk"](pi, ci, c0, w, last)
            si = nxt("sp", 3)
            S.op("scalar", lambda e, pi=pi, si=si, w=w: e.activation(
                out=spb[si][:, 0:w], in_=pA[pi][:, 0:w], func=AF.Exp, scale=-1.0),
                reads=[("pA", pi)], writes=[("sp", si)], extra=u.get("extra", ()))
            S.op("scalar", lambda e, si=si, par=par, c0=c0, w=w: e.activation(
                out=nlbS[par][:, 1 + c0:1 + c0 + w], in_=spb[si][:, 0:w], func=AF.Ln, bias=1.0, scale=1.0),
                reads=[("sp", si)], writes=[("nlb", par, ci)])

    def pr_s1b(u):
        par = u["par"]
        nk = u["nk"]
        chunks = u["chunks"]
        for ci, (c0, w) in enumerate(chunks):
            pi = u["pis"][ci]
            rk = [("pA", pi), ("nlb", par, ci)] + ([("nlb", par, ci - 1), ("u", par, ci - 1)] if ci else [("nlb0", par)])
            if ci == 0:
                S.op("vector", lambda e, pi=pi, par=par, w=w: e.tensor_tensor_scan(
                    out=ubuf[par][:, 0:w], data0=pA[pi][:, 0:w], data1=nlbS[par][:, 0:w], initial=0.0,
                    op0=ALU.add, op1=ALU.add), reads=rk, writes=[("u", par, ci)])
            else:
                S.op("vector", lambda e, pi=pi, par=par, c0=c0, w=w: e.tensor_tensor_scan(
                    out=ubuf[par][:, c0:c0 + w], data0=pA[pi][:, 0:w], data1=nlbS[par][:, c0:c0 + w],
                    initial=ubuf[par][:, c0 - 1:c0], op0=ALU.add, op1=ALU.add), reads=rk, writes=[("u", par, ci)])
        lastc = len(chunks) - 1
        allk = [("u", par, ci) for ci in range(len(chunks))] + [("nlb", par, ci) for ci in range(len(chunks))]
        S.op("vector", lambda e, par=par, nk=nk: e.tensor_tensor(
            out=twin[:], in0=ubuf[par][:, nk - 129:nk - 1], in1=nlbS[par][:, nk - 128:nk], op=ALU.add),
            reads=allk, writes=["twin"])
        S.op("vector", lambda e, par=par: e.scalar_tensor_tensor(
            out=tjunk[:], in0=twin[:], scalar=1.0, in1=nident_f[:], op0=ALU.mult, op1=ALU.mult,
            accum_out=negT[:, par:par + 1]),
            reads=["twin", "nident_f"], writes=["tjunk", ("negT", par)])

    def pr_s2(u):
        par = u["par"]
        for ci, (c0, w) in enumerate(u["chunks"]):
            S.op("scalar", lambda e, par=par, c0=c0, w=w: e.activation(
                out=wfull[par][:, c0:c0 + w], in_=ubuf[par][:, c0:c0 + w], func=AF.Exp,
                bias=negT[:, par:par + 1], scale=1.0),
                reads=[("u", par, ci), ("negT", par)], writes=[("wf", par, ci)])

    def run_pipeline(units):
        n = len(units)
        for k_ in range(n + 2):
            if k_ < n:
                units[k_]["s1a"](units[k_])
            if k_ >= 2:
                attn_s3(units[k_ - 2])
            if k_ < n:
                units[k_]["s1b"](units[k_])
            if 1 <= k_ <= n:
                units[k_ - 1]["s2"](units[k_ - 1])
            if k_ >= 2:
                attn_s3_fin(units[k_ - 2])

    def load_cache(s, extra=()):
        bi = s % 2
        S.dma("gpsimd", f"ckb{bi}", ckb[bi][:], ck[s].rearrange("(t p) f -> p t f", p=128),
              writes=[("ckb", bi)], extra=extra)
        S.dma("gpsimd", f"cvb{bi}", cvb[bi][:], cv[s].rearrange("(t p) f -> p t f", p=128),
              writes=[("cvb", bi)], extra=extra)

    def make_prompt_unit(h, i, par):
        nk = 1024 + 128 * (i + 1)
        chunks = [(c0, min(512, nk - c0)) for c0 in range(0, nk, 512)]
        tcols = slice(i * 128, (i + 1) * 128)
        u = {"par": par, "nk": nk, "chunks": chunks, "s1a": pr_s1a, "s1b": pr_s1b, "s2": pr_s2}

        def qk(pi, ci, c0, w, last):
            rk = [("qT", h), ("kT", h)]
            if not last:
                S.op("tensor", lambda e: e.matmul(pA[pi][:, 0:w], lhsT=qT[:, h, tcols], rhs=kT[:, h, c0:c0 + w],
                                                  start=True, stop=True),
                     reads=rk, writes=[("pA", pi)], inc=True)
                return
            wd = w - 128
            if wd > 0:
                S.op("tensor", lambda e: e.matmul(pA[pi][:, 0:wd], lhsT=qT[:, h, tcols], rhs=kT[:, h, c0:c0 + wd],
                                                  start=True, stop=True),
                     reads=rk, writes=[("pA", pi)], inc=False)
            S.op("tensor", lambda e: e.matmul(pA[pi][:, wd:w], lhsT=qT[:, h, tcols], rhs=kT[:, h, c0 + wd:c0 + w],
                                              start=True, stop=False),
                 reads=rk, writes=[("pA", pi)], inc=False)
            S.op("tensor", lambda e: e.matmul(pA[pi][:, wd:w], lhsT=maskTd[:], rhs=ident_bf[:],
                                              start=False, stop=True),
                 reads=["maskTd", "ident_bf"], writes=[("pA", pi)], inc=True)

        state = {"blk": 0}

        def pv(ci, c0, nb):
            oi = u["oi"]
            for b in range(nb):
                blk = c0 // 128 + b
                lastb = (blk == nk // 128 - 1)
                S.op("tensor", lambda e, blk=blk, lastb=lastb: e.matmul(
                    pTf[oi][:, 0, :], lhsT=v_hm[:, h, blk, :], rhs=wTfull[:, blk, :],
                    start=(blk == 0), stop=lastb),
                    reads=[("wTfull", ci), ("v", blk)], writes=[("pTf", oi)], inc=(lastb or b == nb - 1))

        def fin(oi):
            S.op("vector", lambda e: e.tensor_tensor(out=mixsb[:, h, tcols], in0=pTf[oi][:, 0, :],
                                                     in1=gsb[:, h, tcols], op=ALU.mult),
                 reads=[("pTf", oi), ("gsb", h)], writes=[("mixsb", h)])
        u["qk"] = qk; u["pv"] = pv; u["fin"] = fin
        return u

    for i in range(2):
        S.op("gpsimd", lambda e, i=i: e.memset(nlbS[i][:, 0:1], 0.0), writes=[("nlb0", i)], extra=t_proj_done)
    units = []
    n = 0
    for h in range(NH):
        for i in range(8):
            units.append(make_prompt_unit(h, i, n % 2))
            n += 1
    units[0]["extra"] = t_proj_done
    units[1]["extra"] = t_proj_done
    units[4 * 8 - 1]["post"] = (lambda: load_cache(0, extra=S.snapshot()))
    run_pipeline(units)
    t_attn_done = S.snapshot()

    def issue_wo(cc, extra=()):
        S.dma("gpsimd", f"wo{cc % 2}", wo[cc % 2][:].rearrange("p c f -> p (c f)"), w_out[cc],
              writes=[("wo", cc % 2)], extra=extra)

    def make_sample_unit(s, g, par):
        nk = 1152
        chunks = [(0, 512), (512, 512), (1024, 128)]
        bi = s % 2
        qcols = slice(SMP0 + 32 * s, SMP0 + 32 * s + 32)
        scols = slice(32 * s, 32 * s + 32)
        u = {"par": par, "nk": nk, "chunks": chunks, "e": esb, "P": psb, "ek": "es", "pk": "Ps",
             "s1a": attn_s1a, "s1b": attn_s1b, "s2": attn_s2}

        def qk(pi, ci, c0, w, last):
            for j in range(4):
                h = 4 * g + j
                rhs = kTp[:, h, c0:c0 + w] if ci < 2 else kT_s[:, h, :]
                rk = ["q_s", ("kTp", h)] if ci < 2 else ["q_s", ("kT_s", h)]
                S.op("tensor", lambda e, j=j, h=h, rhs=rhs: e.matmul(
                    pA[pi][32 * j:32 * j + 32, 0:w], lhsT=q_s[:, h, scols], rhs=rhs,
                    start=True, stop=(not last), tile_position=(0, 32 * j), skip_group_check=True),
                    reads=rk, writes=[("pA", pi)], inc=(j == 3 and not last))
            if last:
                S.op("tensor", lambda e: e.matmul(pA[pi][:, 0:128], lhsT=maskTs[s][:], rhs=ident_bf[:],
                                                  start=False, stop=True, skip_group_check=True),
                     reads=[("maskTs", s), "ident_bf"], writes=[("pA", pi)], inc=True)

        def pv(ci, c0, nb):
            pass

        def fin(oi):
            for j in range(4):
                h = 4 * g + j
                for blk in range(9):
                    if blk < 8:
                        lhsT = cvb[bi][:, blk, h * 128:(h + 1) * 128]
                        rk = [("cvb", bi)]
                    else:
                        lhsT = v_s[:, h * 128:(h + 1) * 128]
                        rk = [("v_s", h)]
                    S.op("tensor", lambda e, j=j, blk=blk, lhsT=lhsT: e.matmul(
                        pTf[oi][:, 0, 32 * j:32 * j + 32], lhsT=lhsT, rhs=wTfull[:, blk, 32 * j:32 * j + 32],
                        start=(blk == 0), stop=(blk == 8), skip_group_check=True),
                        reads=rk + [("wTfull", 0), ("wTfull", 1), ("wTfull", 2)], writes=[("pTf", oi)],
                        inc=(blk == 8 and j == 3))
            S.op("vector", lambda e: e.tensor_tensor(
                out=mixsb[:, 4 * g:4 * g + 4, qcols],
                in0=pTf[oi][:, 0, :].rearrange("p (j q) -> p j q", j=4),
                in1=g_s[:, 4 * g:4 * g + 4, scols], op=ALU.mult),
                reads=[("pTf", oi), "g_s"],
                writes=[("mixsb", 4 * g + j) for j in range(4)])
        u["qk"] = qk; u["pv"] = pv; u["fin"] = fin
        return u

    def transpose_kcache(s, g):
        bi = s % 2
        ex = t_attn_done if s == 0 else ()
        for h in range(4 * g, 4 * g + 4):
            for half in range(2):
                ti = nxt("pTb", 2)
                for b in range(4):
                    t = half * 4 + b
                    S.op("tensor", lambda e, t=t, b=b, ti=ti, h=h: e.transpose(
                        out=pTb[ti][:, b, :], in_=ckb[bi][:, t, h * 128:(h + 1) * 128], identity=ident_bf[:]),
                        reads=[("ckb", bi), "ident_bf"], writes=[("pTb", ti)], inc=(b == 3))
                dst = kTp[:, h, half * 512:(half + 1) * 512].rearrange("p (b k) -> p b k", b=4)
                S.op("vector", lambda e, dst=dst, ti=ti: e.tensor_copy(out=dst, in_=pTb[ti][:, 0:4, :]),
                     reads=[("pTb", ti)], writes=[("kTp", h)], extra=ex)

    load_cache(1, extra=t_attn_done)
    S.op("gpsimd", lambda e: e.tensor_copy(out=q_s[:], in_=qT[:, :, SMP0:SMP0 + 128]),
         reads=[("qT", h) for h in range(NH)], writes=["q_s"], extra=t_attn_done)
    S.op("gpsimd", lambda e: e.tensor_copy(out=g_s[:], in_=gsb[:, :, SMP0:SMP0 + 128]),
         reads=[("gsb", h) for h in range(NH)], writes=["g_s"], extra=t_attn_done)
    t_r1_free = S.snapshot()
    for i in range(2):
        S.op("gpsimd", lambda e, i=i: e.memset(psb[i][:, 0:1], 0.0), writes=[("Ps", i)], extra=t_attn_done)
    sunits = []
    for s in range(4):
        us = [make_sample_unit(s, g, g) for g in range(2)]
        us[0]["pre"] = (lambda s=s: transpose_kcache(s, 0))
        us[1]["pre"] = (lambda s=s: transpose_kcache(s, 1))
        if s == 0:
            us[1]["post"] = (lambda: load_cache(2))
        elif s == 1:
            us[1]["post"] = (lambda: (load_cache(3), issue_wo(0, extra=t_r1_free), issue_wo(1, extra=t_r1_free)))
        sunits += us
    sunits[0]["extra"] = t_attn_done
    sunits[1]["extra"] = t_attn_done
    run_pipeline(sunits)
    t_samp_done = S.snapshot()

    for tt in range(9):
        r0 = TC + tt * 128
        S.dma("sync", f"xres{tt}", yv(tt), xa[r0:r0 + 128, :], writes=[("yacc", tt)], extra=t_samp_done)
    S.dma("sync", "const2", fgb[:], final_g.partition_broadcast(128), writes=["fgb"], extra=t_samp_done)

    S.group_begin("sync", "yout")

    def tokcols(tt):
        return slice(tt * 128, (tt + 1) * 128) if tt < 8 else slice(SMP0, SMP0 + 128)

    for cc in range(4):
        for tt in range(9):
            pi = nxt("pA", 4)
            tc_ = tokcols(tt)
            for ec in range(16):
                lhsT = mixsb[:, ec, tc_] if ec < 8 else mixcv[:, ec - 8, tc_]
                rk = [("mixsb", ec)] if ec < 8 else [("mixcv", ec - 8)]
                S.op("tensor", lambda e, pi=pi, lhsT=lhsT, ec=ec, cc=cc: e.matmul(
                    pA[pi][:, 0:512], lhsT=lhsT, rhs=wo[cc % 2][:, ec, :], start=(ec == 0), stop=(ec == 15)),
                    reads=rk + [("wo", cc % 2)], writes=[("pA", pi)], inc=(ec == 15),
                    extra=(t_samp_done if (cc == 0 and tt == 0 and ec == 0) else ()))
            S.op("vector", lambda e, pi=pi, tt=tt, cc=cc: e.tensor_tensor(
                out=yv(tt, cc * 512, (cc + 1) * 512), in0=pA[pi][:, 0:512],
                in1=yv(tt, cc * 512, (cc + 1) * 512), op=ALU.add),
                reads=[("pA", pi), ("yacc", tt)], writes=[("yacc", tt)])
            if cc == 3:
                st = 17 + tt
                S.op("scalar", lambda e, tt=tt, st=st: e.activation(
                    out=junk2[:], in_=yv(tt), func=AF.Square, accum_out=stat[:, st:st + 1]),
                    reads=[("yacc", tt)], writes=["junk2", ("ss", st)])
                S.op("scalar", lambda e, st=st: e.activation(
                    out=stat[:, 32 + st:33 + st], in_=stat[:, st:st + 1], func=AF.Ln, scale=1.0 / D, bias=EPS),
                    reads=[("ss", st)], writes=[("lr", st)])
                S.op("scalar", lambda e, st=st: e.activation(
                    out=stat[:, 64 + st:65 + st], in_=stat[:, 32 + st:33 + st], func=AF.Exp, scale=-0.5),
                    reads=[("lr", st)], writes=[("rr", st)])
                S.op("vector", lambda e, tt=tt, st=st: e.scalar_tensor_tensor(
                    out=yv(tt), in0=yv(tt), scalar=stat[:, 64 + st:65 + st], in1=fgb[:],
                    op0=ALU.mult, op1=ALU.mult),
                    reads=[("yacc", tt), ("rr", st), "fgb"], writes=[("yacc", tt)])
                dst = yp[tt * 128:(tt + 1) * 128, :] if tt < 8 else ys[:, :]
                S.dma("sync", "yout", dst, yv(tt), reads=[("yacc", tt)])
        if cc + 2 < 4:
            issue_wo(cc + 2)

    S.group_end("yout")
    return S


_PROGRAM = None


def _get_program():
    global _PROGRAM
    if _PROGRAM is None:
        _PROGRAM = build_program()
    return _PROGRAM


def kernel(x_prompt, x_sample, cache_k, cache_v, state_conv, norm_g, w_in, conv_w, w_out, final_g):
    f = lambda a: np.ascontiguousarray(np.asarray(a, dtype=np.float32))
    x_prompt, x_sample = f(x_prompt), f(x_sample)
    cache_k, cache_v, state_conv = f(cache_k), f(cache_v), f(state_conv)
    norm_g, w_in, conv_w, w_out, final_g = f(norm_g), f(w_in), f(conv_w), f(w_out), f(final_g)
    nc = build_program()
    w3 = w_in[0].reshape(NDC, 128, 8192)
    w_in_r = np.empty((64, 128, NDC * 128), np.float32)
    for ci, f in enumerate(_forder_cols()):
        w_in_r[ci] = w3[:, :, f * 128:(f + 1) * 128].transpose(1, 0, 2).reshape(128, NDC * 128)
    wo3 = w_out[0].reshape(NDC, 128, D)
    w_out_r = np.empty((4, 128, NDC * 512), np.float32)
    for cc in range(4):
        w_out_r[cc] = wo3[:, :, cc * 512:(cc + 1) * 512].transpose(1, 0, 2).reshape(128, NDC * 512)
    in_maps = []
    for c in range(NCORES):
        b, half = c // 2, c % 2
        xa = np.zeros((TC + TO + TS, D), np.float32)
        if half == 1:
            xa[0:TC] = x_prompt[b, 0:1024]
        xa[TC:TC + TO] = x_prompt[b, half * 1024:(half + 1) * 1024]
        xa[TC + TO:] = x_sample[4 * c:4 * c + 4].reshape(TS, D)
        in_maps.append({
            "xa": xa,
            "ck": np.ascontiguousarray(cache_k[0, 4 * c:4 * c + 4].reshape(4, 1024, 1024)),
            "cv": np.ascontiguousarray(cache_v[0, 4 * c:4 * c + 4].reshape(4, 1024, 1024)),
            "sc": np.ascontiguousarray(state_conv[0, 4 * c:4 * c + 4]),
            "norm_g": norm_g[0], "final_g": final_g, "conv_w": conv_w[0],
            "w_in": w_in_r, "w_out": w_out_r,
        })
    res = run_bass_kernel_spmd(nc, in_maps, core_ids=list(range(NCORES)))
    R = res.results
    y_prompt = np.zeros((4, 2048, D), np.float32)
    y_sample = np.zeros((32, 32, D), np.float32)
    k_prompt = np.zeros((1, 4, 2048, 8, 128), np.float32)
    v_prompt = np.zeros((1, 4, 2048, 8, 128), np.float32)
    conv_prompt = np.zeros((1, 4, 2, 1024), np.float32)
    k_sample = np.zeros((1, 32, 32, 8, 128), np.float32)
    v_sample = np.zeros((1, 32, 32, 8, 128), np.float32)
    conv_sample = np.zeros((1, 32, 2, 1024), np.float32)
    for c in range(NCORES):
        b, half = c // 2, c % 2
        r = R[c]
        sl = slice(half * 1024, (half + 1) * 1024)
        y_prompt[b, sl] = r["yp"]
        k_prompt[0, b, sl] = r["kp"].reshape(1024, 8, 128)
        v_prompt[0, b, sl] = r["vp"].reshape(1024, 8, 128)
        if half == 1:
            conv_prompt[0, b] = r["cp"]
        y_sample[4 * c:4 * c + 4] = r["ys"].reshape(4, 32, D)
        k_sample[0, 4 * c:4 * c + 4] = r["ksn"].reshape(4, 32, 8, 128)
        v_sample[0, 4 * c:4 * c + 4] = r["vsn"].reshape(4, 32, 8, 128)
        conv_sample[0, 4 * c:4 * c + 4] = r["cs"]
    return (y_prompt, y_sample, k_prompt, v_prompt, conv_prompt, k_sample, v_sample, conv_sample)
```
